# Optimizing a Trainium2 kernel written in Bass

```python
import math
import jax, jax.numpy as jnp
from jax import lax
import numpy as np

D_MODEL = 1024
BATCH = 8
SEQ = 2048
DEPTH = 4

HEAD_DIM = 64
GM_GROUPS = 4
GM_CHUNK = 128
GM_WIDTH = GM_GROUPS * HEAD_DIM
DA_HEADS = 4
DA_QK_DIM = HEAD_DIM // 2
DA_V_DIM = HEAD_DIM
DA_QK_WIDTH = DA_HEADS * 2 * DA_QK_DIM
DA_WIDTH = DA_HEADS * DA_V_DIM
NSA_HEADS = 8
NSA_KV_GROUPS = 2
NSA_WIDTH = NSA_HEADS * HEAD_DIM
NSA_KV_WIDTH = NSA_KV_GROUPS * HEAD_DIM
CMP_LEN = 32
CMP_STRIDE = 16
CMP_HIDDEN = 256
SLC_LEN = 64
SLC_TOPK = 16
WIN = 512
FORCE_BONUS = 1000.0
Q_BLOCK = 128
SLC_Q_BLOCK = 64
D_FF = 2816
ROPE_THETA = 10000.0
EPS = 1e-6
N_MOD = 9
D_MIX = GM_WIDTH + DA_WIDTH + NSA_WIDTH
IN_WIDTHS = (GM_WIDTH, GM_WIDTH,
             DA_QK_WIDTH, DA_QK_WIDTH, DA_WIDTH,
             NSA_WIDTH, NSA_KV_WIDTH, NSA_KV_WIDTH, NSA_KV_WIDTH,
             NSA_KV_WIDTH, NSA_KV_WIDTH, NSA_KV_WIDTH, 3 * NSA_HEADS)
D_IN = 2 * GM_WIDTH + 2 * DA_QK_WIDTH + DA_WIDTH + NSA_WIDTH + 6 * NSA_KV_WIDTH + 3 * NSA_HEADS

kernel_name = "hybrid_gmlp_diffattn_nsa_macaron_adaln"


def _rmsnorm(x, g):
    xf = x.astype(jnp.float32)
    y = xf * lax.rsqrt(jnp.mean(xf * xf, axis=-1, keepdims=True) + EPS)
    return (y * g.astype(jnp.float32)).astype(x.dtype)


def _rope(x, pos):
    d = x.shape[-1]
    half = d // 2
    inv = ROPE_THETA ** (-jnp.arange(half, dtype=jnp.float32) / half)
    ang = pos.astype(jnp.float32)[:, None] * inv[None, :]
    cos, sin = jnp.cos(ang), jnp.sin(ang)
    xf = x.astype(jnp.float32)
    x1, x2 = xf[..., :half], xf[..., half:]
    return jnp.concatenate([x1 * cos - x2 * sin, x2 * cos + x1 * sin], axis=-1).astype(x.dtype)


def _masked_softmax(s, mask):
    p = jax.nn.softmax(jnp.where(mask, s, jnp.finfo(jnp.float32).min), axis=-1)
    return jnp.where(mask, p, 0.0)


def _swiglu(h, w_gate, w_up, w_down):
    return (jax.nn.silu(h @ w_gate) * (h @ w_up)) @ w_down


def _gmlp_sgu(u, v, ln_g, w_s, b_s):
    B_, S_, _ = u.shape
    u = jax.nn.gelu(u)
    vf = jax.nn.gelu(v).reshape(B_, S_, GM_GROUPS, HEAD_DIM).astype(jnp.float32)
    mu = jnp.mean(vf, axis=-1, keepdims=True)
    var = jnp.mean((vf - mu) ** 2, axis=-1, keepdims=True)
    vn = ((vf - mu) * lax.rsqrt(var + EPS)).astype(v.dtype) * ln_g.reshape(GM_GROUPS, HEAD_DIM)
    vc = vn.reshape(B_, S_ // GM_CHUNK, GM_CHUNK, GM_GROUPS, HEAD_DIM)
    causal = jnp.tril(jnp.ones((GM_CHUNK, GM_CHUNK), dtype=bool))
    w = jnp.where(causal[None], w_s, jnp.zeros_like(w_s))
    s = jnp.einsum('gts,bnsgd->bntgd', w, vc) + b_s.T[None, None, :, :, None]
    return u * s.reshape(B_, S_, GM_WIDTH)


def _diff_attention(q, k, v, lam, sub_g, lam_init):
    B_, S_, _ = q.shape
    pos = jnp.arange(S_)
    q = _rope(q.reshape(B_, S_, DA_HEADS, 2, DA_QK_DIM).transpose(0, 2, 3, 1, 4), pos)
    k = _rope(k.reshape(B_, S_, DA_HEADS, 2, DA_QK_DIM).transpose(0, 2, 3, 1, 4), pos)
    v = v.reshape(B_, S_, DA_HEADS, DA_V_DIM).transpose(0, 2, 1, 3)
    scale = DA_QK_DIM ** -0.5
    nb = S_ // Q_BLOCK
    qb = q.reshape(B_, DA_HEADS, 2, nb, Q_BLOCK, DA_QK_DIM).transpose(3, 0, 1, 2, 4, 5)

    def block(args):
        qi, i = args
        s = jnp.einsum('bhmqd,bhmkd->bhmqk', qi, k).astype(jnp.float32) * scale
        t = i * Q_BLOCK + jnp.arange(Q_BLOCK)
        p = _masked_softmax(s, t[:, None] >= pos[None, :])
        a = p[:, :, 0] - lam * p[:, :, 1]
        return jnp.einsum('bhqk,bhkd->bhqd', a.astype(v.dtype), v)

    o = lax.map(block, (qb, jnp.arange(nb)))
    o = o.transpose(1, 2, 0, 3, 4).reshape(B_, DA_HEADS, S_, DA_V_DIM)
    o = _rmsnorm(o, sub_g) * (1.0 - lam_init)
    return o.transpose(0, 2, 1, 3).reshape(B_, S_, DA_WIDTH)


def _compress(x, pe, w1, w2):
    S_ = x.shape[2]
    nc = (S_ - CMP_LEN) // CMP_STRIDE + 1
    idx = jnp.arange(nc)[:, None] * CMP_STRIDE + jnp.arange(CMP_LEN)[None, :]
    blocks = x[:, :, idx] + pe
    flat = blocks.reshape(blocks.shape[0], blocks.shape[1], nc, CMP_LEN * HEAD_DIM)
    return jax.nn.silu(flat @ w1) @ w2


def _cmp_to_slc(nc, ns):
    cs = np.arange(nc) * CMP_STRIDE
    ce = cs + CMP_LEN
    bs = np.arange(ns) * SLC_LEN
    be = bs + SLC_LEN
    ov = np.clip(np.minimum(ce[:, None], be[None, :]) - np.maximum(cs[:, None], bs[None, :]), 0, None)
    return (ov / CMP_STRIDE).astype(np.float32)


def _nsa(q, kc, vc, ks, vs, kw, vw, gates, pe, w1, w2):
    B_, S_, _ = q.shape
    G, Hg, Dh = NSA_KV_GROUPS, NSA_HEADS // NSA_KV_GROUPS, HEAD_DIM
    pos = jnp.arange(S_)
    scale = Dh ** -0.5
    q = _rope(q.reshape(B_, S_, G, Hg, Dh).transpose(0, 2, 3, 1, 4), pos)

    def kv(t):
        return t.reshape(B_, S_, G, Dh).transpose(0, 2, 1, 3)

    nc = (S_ - CMP_LEN) // CMP_STRIDE + 1
    cmp_end = jnp.arange(nc) * CMP_STRIDE + CMP_LEN - 1
    k_cmp = _rope(_compress(kv(kc), pe[0], w1[0], w2[0]), cmp_end)
    v_cmp = _compress(kv(vc), pe[1], w1[1], w2[1])
    s = jnp.einsum('bgjtd,bgcd->bgjtc', q, k_cmp).astype(jnp.float32) * scale
    p_cmp = _masked_softmax(s, cmp_end[None, :] <= pos[:, None])
    o_cmp = jnp.einsum('bgjtc,bgcd->bgjtd', p_cmp.astype(q.dtype), v_cmp)

    ns = S_ // SLC_LEN
    topk = min(SLC_TOPK, ns)
    imp = jnp.einsum('bgjtc,cn->bgtn', p_cmp, jnp.asarray(_cmp_to_slc(nc, ns)))
    blk = jnp.arange(ns)[None, :]
    cur = (pos // SLC_LEN)[:, None]
    valid = blk <= cur
    forced = valid & ((blk == 0) | (blk >= cur - 1))
    score = jnp.where(forced, FORCE_BONUS, jnp.where(valid, imp, -1.0))
    top_val, top_idx = lax.top_k(score, topk)
    top_ok = top_val >= 0.0
    k_slc = _rope(kv(ks), pos).reshape(B_, G, ns, SLC_LEN, Dh)
    v_slc = kv(vs).reshape(B_, G, ns, SLC_LEN, Dh)
    nqb = S_ // SLC_Q_BLOCK
    qs = q.reshape(B_, G, Hg, nqb, SLC_Q_BLOCK, Dh).transpose(3, 0, 1, 2, 4, 5)
    ib = top_idx.reshape(B_, G, nqb, SLC_Q_BLOCK, topk).transpose(2, 0, 1, 3, 4)
    okb = top_ok.reshape(B_, G, nqb, SLC_Q_BLOCK, topk).transpose(2, 0, 1, 3, 4)
    gather = jax.vmap(jax.vmap(lambda blocks, ix: blocks[ix]))

    def slc_block(args):
        qi, ix, ok, i = args
        kg = gather(k_slc, ix)
        vg = gather(v_slc, ix)
        t = i * SLC_Q_BLOCK + jnp.arange(SLC_Q_BLOCK)
        kpos = ix[..., None] * SLC_LEN + jnp.arange(SLC_LEN)
        mask = ok[..., None] & (kpos <= t[:, None, None])
        sc = jnp.einsum('bgjqd,bgqnld->bgjqnl', qi, kg).astype(jnp.float32) * scale
        sc = sc.reshape(B_, G, Hg, SLC_Q_BLOCK, topk * SLC_LEN)
        p = _masked_softmax(sc, mask.reshape(B_, G, 1, SLC_Q_BLOCK, topk * SLC_LEN))
        p = p.reshape(B_, G, Hg, SLC_Q_BLOCK, topk, SLC_LEN)
        return jnp.einsum('bgjqnl,bgqnld->bgjqd', p.astype(vg.dtype), vg)

    o_slc = lax.map(slc_block, (qs, ib, okb, jnp.arange(nqb)))
    o_slc = o_slc.transpose(1, 2, 3, 0, 4, 5).reshape(B_, G, Hg, S_, Dh)

    pad = ((0, 0), (0, 0), (WIN, 0), (0, 0))
    k_win = jnp.pad(_rope(kv(kw), pos), pad)
    v_win = jnp.pad(kv(vw), pad)
    nb = S_ // Q_BLOCK
    qw = q.reshape(B_, G, Hg, nb, Q_BLOCK, Dh).transpose(3, 0, 1, 2, 4, 5)

    def win_block(args):
        qi, i = args
        start = i * Q_BLOCK
        kb = lax.dynamic_slice_in_dim(k_win, start, WIN + Q_BLOCK, axis=2)
        vb = lax.dynamic_slice_in_dim(v_win, start, WIN + Q_BLOCK, axis=2)
        t = start + jnp.arange(Q_BLOCK)
        kpos = start - WIN + jnp.arange(WIN + Q_BLOCK)
        dist = t[:, None] - kpos[None, :]
        mask = (kpos[None, :] >= 0) & (dist >= 0) & (dist < WIN)
        sc = jnp.einsum('bgjqd,bgkd->bgjqk', qi, kb).astype(jnp.float32) * scale
        p = _masked_softmax(sc, mask)
        return jnp.einsum('bgjqk,bgkd->bgjqd', p.astype(vb.dtype), vb)

    o_win = lax.map(win_block, (qw, jnp.arange(nb)))
    o_win = o_win.transpose(1, 2, 3, 0, 4, 5).reshape(B_, G, Hg, S_, Dh)

    g = jax.nn.sigmoid(gates.astype(jnp.float32)).reshape(B_, S_, G, Hg, 3).transpose(0, 2, 3, 1, 4).astype(q.dtype)
    o = g[..., 0:1] * o_cmp + g[..., 1:2] * o_slc + g[..., 2:3] * o_win
    return o.transpose(0, 3, 1, 2, 4).reshape(B_, S_, NSA_WIDTH)


def _token_mix(h, w_in, w_out, gm_ln_g, gm_w_s, gm_b_s, da_lambda, da_sub_g, cmp_pe, cmp_w1, cmp_w2, lam_init):
    z = h @ w_in
    pts = np.cumsum(np.array(IN_WIDTHS))[:-1].tolist()
    (gu, gv, dq, dk, dv, nq, nkc, nvc, nks, nvs, nkw, nvw, ng) = jnp.split(z, pts, axis=-1)
    y_a = _gmlp_sgu(gu, gv, gm_ln_g, gm_w_s, gm_b_s)
    lf = da_lambda.astype(jnp.float32)
    lam = jnp.exp(jnp.sum(lf[0] * lf[1])) - jnp.exp(jnp.sum(lf[2] * lf[3])) + lam_init
    y_b = _diff_attention(dq, dk, dv, lam, da_sub_g, lam_init)
    y_c = _nsa(nq, nkc, nvc, nks, nvs, nkw, nvw, ng, cmp_pe, cmp_w1, cmp_w2)
    return jnp.concatenate([y_a, y_b, y_c], axis=-1) @ w_out


def setup_inputs(seed: int = 0) -> dict:
    key = jax.random.key(seed)
    ks = jax.random.split(key, 19)
    f32 = jnp.float32

    def nrm(k, shape, std):
        return std * jax.random.normal(k, shape, f32)

    L = DEPTH
    return {
        "x": nrm(ks[0], (BATCH, SEQ, D_MODEL), 1.0),
        "c": nrm(ks[1], (BATCH, D_MODEL), 1.0),
        "w_ada": nrm(ks[2], (L, D_MODEL, N_MOD * D_MODEL), 0.02),
        "b_ada": nrm(ks[3], (L, N_MOD * D_MODEL), 0.02),
        "norm_g": 1.0 + nrm(ks[4], (L, 3, D_MODEL), 0.05),
        "ffn_w_gate": nrm(ks[5], (L, 2, D_MODEL, D_FF), D_MODEL ** -0.5),
        "ffn_w_up": nrm(ks[6], (L, 2, D_MODEL, D_FF), D_MODEL ** -0.5),
        "ffn_w_down": nrm(ks[7], (L, 2, D_FF, D_MODEL), D_FF ** -0.5),
        "w_in": nrm(ks[8], (L, D_MODEL, D_IN), D_MODEL ** -0.5),
        "w_out": nrm(ks[9], (L, D_MIX, D_MODEL), D_MIX ** -0.5),
        "gm_ln_g": 1.0 + nrm(ks[10], (L, GM_WIDTH), 0.05),
        "gm_w_s": nrm(ks[11], (L, GM_GROUPS, GM_CHUNK, GM_CHUNK), GM_CHUNK ** -0.5),
        "gm_b_s": 1.0 + nrm(ks[12], (L, GM_GROUPS, GM_CHUNK), 0.1),
        "da_lambda": nrm(ks[13], (L, 4, DA_QK_DIM), 0.1),
        "da_sub_g": 1.0 + nrm(ks[14], (L, DA_V_DIM), 0.05),
        "nsa_cmp_pe": nrm(ks[15], (L, 2, CMP_LEN, HEAD_DIM), 0.02),
        "nsa_cmp_w1": nrm(ks[16], (L, 2, CMP_LEN * HEAD_DIM, CMP_HIDDEN), (CMP_LEN * HEAD_DIM) ** -0.5),
        "nsa_cmp_w2": nrm(ks[17], (L, 2, CMP_HIDDEN, HEAD_DIM), CMP_HIDDEN ** -0.5),
        "final_g": 1.0 + nrm(ks[18], (D_MODEL,), 0.05),
    }


def reference(x, c, w_ada, b_ada, norm_g, ffn_w_gate, ffn_w_up, ffn_w_down, w_in, w_out,
              gm_ln_g, gm_w_s, gm_b_s, da_lambda, da_sub_g, nsa_cmp_pe, nsa_cmp_w1, nsa_cmp_w2, final_g):
    c_act = jax.nn.silu(c)
    for l in range(DEPTH):
        mod = (c_act @ w_ada[l] + b_ada[l])[:, None, :]
        sh0, sc0, g0, sh1, sc1, g1, sh2, sc2, g2 = jnp.split(mod, N_MOD, axis=-1)
        lam_init = 0.8 - 0.6 * math.exp(-0.3 * l)
        h = _rmsnorm(x, norm_g[l, 0]) * (1.0 + sc0) + sh0
        x = x + 0.5 * g0 * _swiglu(h, ffn_w_gate[l, 0], ffn_w_up[l, 0], ffn_w_down[l, 0])
        h = _rmsnorm(x, norm_g[l, 1]) * (1.0 + sc1) + sh1
        x = x + g1 * _token_mix(h, w_in[l], w_out[l], gm_ln_g[l], gm_w_s[l], gm_b_s[l], da_lambda[l],
                                da_sub_g[l], nsa_cmp_pe[l], nsa_cmp_w1[l], nsa_cmp_w2[l], lam_init)
        h = _rmsnorm(x, norm_g[l, 2]) * (1.0 + sc2) + sh2
        x = x + 0.5 * g2 * _swiglu(h, ffn_w_gate[l, 1], ffn_w_up[l, 1], ffn_w_down[l, 1])
    return _rmsnorm(x, final_g)
```

```python
import numpy as np
import ml_dtypes
from contextlib import ExitStack
import concourse.bass as bass
import concourse.mybir as mybir
from concourse.bass_utils import run_bass_kernel_spmd

F32 = mybir.dt.float32
BF16 = mybir.dt.bfloat16
AF = mybir.ActivationFunctionType
ALU = mybir.AluOpType
AX = mybir.AxisListType

S_LEN = 2048
D = 1024
DFF = 2816
NFT = 22
DEPTH = 4
EPS = 1e-6
BIG = 1000.0
D_IN = 2584


class Sched:
    KDMA = 8

    def __init__(self, nc, es):
        self.nc = nc
        self.eng = {"pe": nc.tensor, "act": nc.scalar, "dve": nc.vector, "pool": nc.gpsimd, "sp": nc.sync}
        self.sem = {e: es.enter_context(nc.semaphore("s_" + e)) for e in ("pe", "act", "dve", "pool")}
        self.dsem = {q: [es.enter_context(nc.semaphore("d_%s%d" % (q, i))) for i in range(self.KDMA)]
                     for q in ("sp", "pool", "act")}
        self.dn = {q: 0 for q in self.dsem}
        self.dlast = {q: [0] * self.KDMA for q in self.dsem}
        self.cnt = {e: 0 for e in self.sem}
        self.seen = {e: {} for e in self.eng}
        self.last_w = {}
        self.readers = {}
        self.pe_seq = 0
        self.pe_flags = []
        self.nops = 0

    def _resolve(self, tok):
        if tok[0] == "c":
            return ("c" + tok[1], self.sem[tok[1]], tok[2], tok[1])
        if tok[0] == "pe":
            seq = tok[1]
            best = None
            lo, hi = 0, len(self.pe_flags)
            while lo < hi:
                mid = (lo + hi) // 2
                if self.pe_flags[mid][0] >= seq:
                    hi = mid
                else:
                    lo = mid + 1
            assert lo < len(self.pe_flags), "dependency on unflagged trailing PE op"
            best = self.pe_flags[lo][1]
            return ("cpe", self.sem["pe"], best, "pe")
        q, idx, val = tok[1], tok[2], tok[3]
        return ("d%s%d" % (q, idx), self.dsem[q][idx], val, "dma")

    def _wait(self, eng, need):
        E = self.eng[eng]
        for key, (sem, val) in need.items():
            if self.seen[eng].get(key, 0) < val:
                E.wait_ge(sem, val)
                self.seen[eng][key] = val

    def op(self, eng, fn, reads=(), writes=(), sig=True, dma=False):
        deps = []
        for r in reads:
            t = self.last_w.get(r)
            if t is not None:
                deps.append((t, True))
        for w in writes:
            t = self.last_w.get(w)
            if t is not None:
                deps.append((t, False))
            for t in self.readers.get(w, ()):
                deps.append((t, False))
        need = {}
        for tok, raw in deps:
            if tok[0] == "pe" and eng == "pe" and not dma:
                continue
            if tok[0] == "c" and tok[1] == eng and not raw and not dma:
                continue
            key, sem, val, _ = self._resolve(tok)
            if key not in need or need[key][1] < val:
                need[key] = (sem, val)
        self._wait(eng, need)
        ins = fn(self.eng[eng])
        self.nops += 1
        if dma:
            q = eng
            k = self.dn[q]
            idx = k % self.KDMA
            val = 16 * (k // self.KDMA + 1)
            ins.then_inc(self.dsem[q][idx], 16)
            self.dn[q] += 1
            self.dlast[q][idx] = val
            tok = ("d", q, idx, val)
        elif eng == "pe":
            self.pe_seq += 1
            if sig:
                self.cnt["pe"] += 1
                ins.then_inc(self.sem["pe"], 1)
                self.pe_flags.append((self.pe_seq, self.cnt["pe"]))
            tok = ("pe", self.pe_seq)
        else:
            self.cnt[eng] += 1
            ins.then_inc(self.sem[eng], 1)
            tok = ("c", eng, self.cnt[eng])
        for r in reads:
            self.readers.setdefault(r, []).append(tok)
        for w in writes:
            self.last_w[w] = tok
            self.readers[w] = []
        return tok

    def barrier(self, engines=("pe", "act", "dve", "pool", "sp")):
        assert not self.pe_flags or self.pe_flags[-1][0] == self.pe_seq, "last PE op must be flagged before barrier"
        for e in engines:
            need = {}
            for c in ("pe", "act", "dve", "pool"):
                if self.cnt[c] > 0 and c != e:
                    need["c" + c] = (self.sem[c], self.cnt[c])
            for q in self.dsem:
                for i in range(self.KDMA):
                    if self.dlast[q][i] > 0:
                        need["d%s%d" % (q, i)] = (self.dsem[q][i], self.dlast[q][i])
            self._wait(e, need)
        self.last_w.clear()
        self.readers.clear()

    def final_wait(self):
        need = {}
        for c in ("pe", "act", "dve", "pool"):
            if self.cnt[c] > 0:
                need["c" + c] = (self.sem[c], self.cnt[c])
        for q in self.dsem:
            for i in range(self.KDMA):
                if self.dlast[q][i] > 0:
                    need["d%s%d" % (q, i)] = (self.dsem[q][i], self.dlast[q][i])
        self._wait("sp", need)


def _rope_tables(half, rows, positions):
    inv = (np.float32(10000.0) ** (-(np.arange(half, dtype=np.float32)) / np.float32(half))).astype(np.float32)
    ang = (positions.astype(np.float32)[None, :] * inv[:, None]).astype(np.float32)
    cos = np.cos(ang.astype(np.float64)).astype(np.float32)
    sin = np.sin(ang.astype(np.float64)).astype(np.float32)
    r = np.arange(rows)
    i = r % half
    sign = np.where((r % (2 * half)) < half, -1.0, 1.0).astype(np.float32)
    return np.ascontiguousarray(cos[i]), np.ascontiguousarray(sin[i] * sign[:, None])


def _consts():
    c = {}
    pos = np.arange(S_LEN)
    cd, sd = _rope_tables(16, 128, pos)
    cn, sn = _rope_tables(32, 128, pos)
    c["ropeD"] = np.stack([cd, sd], 1)
    c["ropeN"] = np.stack([cn, sn], 1)
    cend = np.arange(127) * 16 + 31
    cc, sc = _rope_tables(32, 64, cend)
    c["ropeC"] = np.ascontiguousarray(np.stack([cc, sc], 1)).astype(np.float32)
    k = np.arange(128)[:, None]
    t = np.arange(128)[None, :]
    bf = ml_dtypes.bfloat16
    tri = np.zeros((128, 3, 128), np.float32)
    tri[:, 0, :] = np.where(k <= t, 0.0, -BIG)
    tri[:, 1, :] = np.where(k > t, 0.0, -BIG)
    tri[:, 2, :] = np.eye(128)
    c["tri"] = tri
    c["tri01"] = np.where(k <= t, 1.0, 0.0).astype(np.float32)
    cb = np.full((128, S_LEN), -BIG, np.float32)
    cb[:127] = np.where(cend[:, None] <= pos[None, :], 0.0, -BIG)
    c["cmpbias"] = cb
    E = np.zeros((32, S_LEN), np.float32)
    E[pos // 64, pos] = 1.0
    c["Emat"] = E
    cs = np.arange(127) * 16
    ce = cs + 32
    bs = np.arange(32) * 64
    be = bs + 64
    ov = np.clip(np.minimum(ce[:, None], be[None, :]) - np.maximum(cs[:, None], bs[None, :]), 0, None)
    M = np.zeros((128, 32), np.float32)
    M[:127] = (ov / 16).astype(np.float32)
    c["Mov"] = M
    tt = np.arange(S_LEN)
    cur = (tt // 64)[:, None]
    blk = np.arange(32)[None, :]
    valid = blk <= cur
    forced = valid & ((blk == 0) | (blk >= cur - 1))
    m01 = (valid & ~forced).astype(np.float32)
    add = np.where(forced, 1000.0, np.where(valid, 0.0, -1.0)).astype(np.float32)
    c["tk"] = np.ascontiguousarray(
        np.stack([m01.reshape(16, 128, 32).transpose(1, 0, 2), add.reshape(16, 128, 32).transpose(1, 0, 2)], 1))
    return c


def _win_cols():
    o = {}
    offs = np.cumsum([0, 256, 256, 256, 256, 256, 512, 128, 128, 128, 128, 128, 128, 24])
    names = ["gu", "gv", "dq", "dk", "dv", "nq", "nkc", "nvc", "nks", "nvs", "nkw", "nvw", "ng"]
    off = {n: int(offs[i]) for i, n in enumerate(names)}
    sw32 = (np.arange(32) + 16) % 32
    sw64 = (np.arange(64) + 32) % 64
    A_tok = np.concatenate([off["gu"] + np.arange(256), off["gv"] + np.arange(256), off["dv"] + np.arange(256)])
    dq = off["dq"] + np.arange(256)
    dk = off["dk"] + np.arange(256)
    dq_s = off["dq"] + (np.arange(8)[:, None] * 32 + sw32[None, :]).reshape(-1)
    dk_s = off["dk"] + (np.arange(8)[:, None] * 32 + sw32[None, :]).reshape(-1)
    A_feat = np.concatenate([dq, dq_s, dk, dk_s])
    o["A_tok"], o["A_feat"] = A_tok, A_feat
    for g in range(2):
        q = off["nq"] + g * 256 + np.arange(256)
        q_s = off["nq"] + g * 256 + (np.arange(4)[:, None] * 64 + sw64[None, :]).reshape(-1)
        ks = off["nks"] + g * 64 + np.arange(64)
        ks_s = off["nks"] + g * 64 + sw64
        kw = off["nkw"] + g * 64 + np.arange(64)
        kw_s = off["nkw"] + g * 64 + sw64
        kc = off["nkc"] + g * 64 + np.arange(64)
        vc = off["nvc"] + g * 64 + np.arange(64)
        o["N_feat%d" % g] = np.concatenate([q, q_s, ks, ks_s, kw, kw_s, kc, kc, vc, vc])
        vs = off["nvs"] + g * 64 + np.arange(64)
        vw = off["nvw"] + g * 64 + np.arange(64)
        gt = off["ng"] + g * 12 + np.arange(12)
        o["N_tok%d" % g] = np.concatenate([vs, vw, gt])
    return o


def _pk(w, ncols):
    return np.ascontiguousarray(w.reshape(8, 128, ncols).transpose(1, 0, 2))


def _prep_shared(inp, depth):
    f = np.float32
    L = depth
    sh = {}
    cols = _win_cols()
    wg = inp["ffn_w_gate"][:L].reshape(L, 2, 8, 128, NFT, 128).transpose(0, 1, 4, 3, 2, 5)
    sh["wg"] = np.ascontiguousarray(wg).reshape(L * 2 * NFT, 128, 1024)
    wu = inp["ffn_w_up"][:L].reshape(L, 2, 8, 128, NFT, 128).transpose(0, 1, 4, 3, 2, 5)
    sh["wu"] = np.ascontiguousarray(wu).reshape(L * 2 * NFT, 128, 1024)
    wd = inp["ffn_w_down"][:L].reshape(L, 2, NFT, 128, 8, 128).transpose(0, 1, 4, 3, 2, 5)
    sh["wd"] = np.ascontiguousarray(wd).reshape(L * 2 * 8, 128, NFT * 128)
    sh["wada"] = np.ascontiguousarray(inp["w_ada"][:L]).reshape(L * 1024, 9216)
    sh["bada"] = np.ascontiguousarray(inp["b_ada"][:L].reshape(L, 72, 128).transpose(2, 0, 1)).reshape(128, L * 72)
    sh["normg"] = np.ascontiguousarray(inp["norm_g"][:L].reshape(L, 24, 128).transpose(2, 0, 1)).reshape(128, L * 24)
    sh["finalg"] = np.ascontiguousarray(inp["final_g"].reshape(8, 128).T)
    win = inp["w_in"][:L]
    sh["wA_tok"] = np.stack([_pk(win[l][:, cols["A_tok"]], 768) for l in range(L)]).reshape(L * 128, 8, 768)
    sh["wA_feat"] = np.stack([_pk(win[l][:, cols["A_feat"]], 1024) for l in range(L)]).reshape(L * 128, 8, 1024)
    sh["wN_feat"] = np.stack([_pk(win[l][:, cols["N_feat%d" % g]], 1024) for l in range(L) for g in range(2)]).reshape(L * 2 * 128, 8, 1024)
    sh["wN_tok"] = np.stack([_pk(win[l][:, cols["N_tok%d" % g]], 140) for l in range(L) for g in range(2)]).reshape(L * 2 * 128, 8, 140)
    sh["wout"] = np.stack([_pk(inp["w_out"][l], 1024) for l in range(L)]).reshape(L * 128, 8, 1024)
    sh["gmlng"] = np.ascontiguousarray(np.broadcast_to(inp["gm_ln_g"][:L].reshape(1, L * 256), (128, L * 256)))
    sh["gmws"] = np.ascontiguousarray(inp["gm_w_s"][:L].transpose(0, 3, 1, 2)).reshape(L * 128, 4, 128)
    sh["gmbs"] = np.ascontiguousarray(inp["gm_b_s"][:L].transpose(0, 2, 1)).reshape(L * 128, 4)
    sh["dalam"] = np.ascontiguousarray(np.broadcast_to(inp["da_lambda"][:L].reshape(1, L * 128), (128, L * 128)))
    sh["dasubg"] = np.ascontiguousarray(np.broadcast_to(inp["da_sub_g"][:L].reshape(1, L * 64), (128, L * 64)))
    pe = inp["nsa_cmp_pe"][:L].reshape(L, 2, 16, 2, 64).transpose(0, 1, 3, 4, 2)
    sh["cmppe"] = np.ascontiguousarray(pe).reshape(L * 2 * 128, 16)
    w1 = inp["nsa_cmp_w1"][:L].reshape(L, 2, 16, 128, 256).transpose(0, 1, 3, 2, 4)
    sh["cmpw1"] = np.ascontiguousarray(w1).reshape(L * 2 * 128, 16, 256)
    w2 = inp["nsa_cmp_w2"][:L].reshape(L, 2, 2, 128, 64).transpose(0, 1, 3, 2, 4)
    sw64 = (np.arange(64) + 32) % 64
    w2k = np.concatenate([w2[:, 0], w2[:, 0][..., sw64]], -1)
    sh["cmpw2k"] = np.ascontiguousarray(w2k).reshape(L * 128, 2, 128)
    sh["cmpw2v"] = np.ascontiguousarray(w2[:, 1]).reshape(L * 128, 2, 64)
    for k, v in _consts().items():
        sh[k] = v
    return {k: np.ascontiguousarray(v, dtype=f) for k, v in sh.items()}


def build(depth=DEPTH, shapes=None, mix=True, final=True):
    nc = bass.Bass("TRN2", target_bir_lowering=False)
    dr = {}
    for name, shp in shapes.items():
        dr[name] = nc.dram_tensor(name, list(shp), F32, kind="ExternalInput").ap()
    outT_d = nc.dram_tensor("outT", [D, S_LEN], F32, kind="ExternalOutput").ap()

    with ExitStack() as es:
        S = Sched(nc, es)

        def sb(name, shape, dt):
            return es.enter_context(nc.sbuf_tensor("sb_" + name, list(shape), dt))

        xT = sb("xT", [128, 8, S_LEN], F32)
        hT = sb("hT", [128, 8, S_LEN], BF16)
        SCRB = 100 * 1024
        scr = sb("scr", [128, SCRB // 2], BF16)
        modT = sb("modT", [128, 72], F32)
        modA = sb("modA", [128, 24], F32)
        modG = sb("modG", [128, 24], F32)
        normg = sb("normg", [128, depth * 24], F32)
        bada = sb("bada", [128, depth * 72], F32)
        finalg = sb("finalg", [128, 8], F32)
        cact = sb("cact", [128, 8], F32)
        cactb = sb("cactb", [128, 8], BF16)
        ones_b = sb("ones_b", [128, 128], BF16)
        epsT = sb("epsT", [128, 1], F32)
        tri = sb("tri", [128, 3, 128], BF16)
        ps = [es.enter_context(nc.psum_tensor("ps%d" % i, [128, 512], F32)) for i in range(8)]

        def carve(off, shape, dt):
            n = int(np.prod(shape))
            bpe = 4 if dt == F32 else 2
            assert off % 4 == 0 and off + n * bpe <= SCRB, (off, shape)
            v = scr[:, off // 2: off // 2 + n * bpe // 2]
            if dt == F32:
                v = v.bitcast(F32)
            if len(shape) == 2:
                return v.rearrange("p (a b) -> p a b", a=shape[0])
            if len(shape) == 3:
                return v.rearrange("p (a b c) -> p a b c", a=shape[0], b=shape[1])
            return v

        def mm(out, lhsT, rhs, start, stop, reads, writes, sig=False):
            return S.op("pe", lambda e: e.matmul(out, lhsT=lhsT, rhs=rhs, start=start, stop=stop,
                                                 skip_group_check=True), reads, writes, sig=sig)

        def act(out, in_, func, reads, writes, scale=1.0, bias=None, accum=None):
            kw = {}
            if bias is not None:
                kw["bias"] = bias
            if accum is not None:
                kw["accum_out"] = accum
            return S.op("act", lambda e: e.activation(out=out, in_=in_, func=func, scale=scale, **kw), reads, writes)

        def tt(out, in0, in1, op, reads, writes, eng="dve"):
            return S.op(eng, lambda e: e.tensor_tensor(out=out, in0=in0, in1=in1, op=op), reads, writes)

        def ts(out, in0, s1, s2, op0, op1, reads, writes, eng="dve"):
            if op1 is None:
                return S.op(eng, lambda e: e.tensor_scalar(out=out, in0=in0, scalar1=s1, scalar2=None, op0=op0), reads, writes)
            return S.op(eng, lambda e: e.tensor_scalar(out=out, in0=in0, scalar1=s1, scalar2=s2, op0=op0, op1=op1), reads, writes)

        def stt(out, in0, scalar, in1, op0, op1, reads, writes):
            return S.op("dve", lambda e: e.scalar_tensor_tensor(out=out, in0=in0, scalar=scalar, in1=in1, op0=op0, op1=op1), reads, writes)

        def cp(out, in_, reads, writes, eng="dve"):
            if eng == "act":
                return S.op("act", lambda e: e.activation(out=out, in_=in_, func=AF.Copy), reads, writes)
            return S.op(eng, lambda e: e.tensor_copy(out=out, in_=in_), reads, writes)

        def dma(q, out, in_, reads, writes):
            return S.op(q, lambda e: e.dma_start(out=out, in_=in_), reads, writes, dma=True)

        def memset(ap, val, writes, eng="dve"):
            return S.op(eng, lambda e: e.memset(ap, val), (), writes)

        for c in range(8):
            for h in range(2):
                dma("sp", xT[:, c, h * 1024:(h + 1) * 1024], dr["xT"][c * 128:(c + 1) * 128, h * 1024:(h + 1) * 1024],
                    (), [("xT", c, 2 * h), ("xT", c, 2 * h + 1)])
        dma("sp", cact[:], dr["cT"][:, :], (), ["cact"])
        dma("sp", normg[:], dr["normg"][:, :], (), ["normg"])
        dma("sp", bada[:], dr["bada"][:, :], (), ["bada"])
        dma("sp", finalg[:], dr["finalg"][:, :], (), ["finalg"])
        dma("pool", tri[:], dr["tri"][:, :, :], (), ["tri"])
        memset(ones_b[:], 1.0, ["ones"])
        memset(epsT[:], EPS, ["eps"])
        act(cact[:], cact[:], AF.Silu, ["cact"], ["cact"])
        cp(cactb[:], cact[:], ["cact"], ["cactb"])

        def mod_layer(l):
            wada_ring = [carve(i * 4608, [1, 2304], BF16)[:, 0, :] for i in range(3)]
            nslot = 0
            for kc in range(8):
                for q in range(4):
                    slot = wada_ring[nslot % 3]
                    key = ("wada", nslot % 3)
                    nslot += 1
                    dma("pool", slot[:], dr["wada"][l * 1024 + kc * 128: l * 1024 + (kc + 1) * 128, q * 2304:(q + 1) * 2304],
                        (), [key])
                    for t in range(18):
                        ct = q * 18 + t
                        first = (kc == 0 and ct == 0)
                        last = (kc == 7 and ct == 71)
                        mm(ps[0][:, ct:ct + 1], slot[:, t * 128:(t + 1) * 128], cactb[:, kc:kc + 1], first, kc == 7,
                           [key, "cactb"], [("ps", 0)], sig=(t == 17))
            tt(modT[:], ps[0][:, 0:72], bada[:, l * 72:(l + 1) * 72], ALU.add, [("ps", 0), "bada"], ["modT"])
            for j in range(3):
                stt(modA[:, j * 8:(j + 1) * 8], modT[:, (3 * j + 1) * 8:(3 * j + 2) * 8], 1.0,
                    normg[:, l * 24 + j * 8: l * 24 + (j + 1) * 8], ALU.add, ALU.mult, ["modT", "normg"], [("modA", j)])
                ts(modG[:, j * 8:(j + 1) * 8], modT[:, (3 * j + 2) * 8:(3 * j + 3) * 8], 0.5 if j != 1 else 1.0, None,
                   ALU.mult, None, ["modT"], [("modG", j)])

        def norm_mod(j, chunks, A_ap, B_ap, Akey, Bkey, o_sq, o_rstd, o_tmp, out_fn=None):
            sq = [carve(o_sq + i * 1024, [1, 512], BF16)[:, 0, :] for i in range(2)]
            rstd = [carve(o_rstd + i * 2048, [1, 512], F32)[:, 0, :] for i in range(2)]
            tmp = [carve(o_tmp + i * 2048, [1, 512], F32)[:, 0, :] for i in range(2)]
            n = 0
            for ci, tc in enumerate(chunks):
                tsl = slice(tc * 512, (tc + 1) * 512)
                pb = ps[ci % 2]
                for c in range(8):
                    s_ = sq[n % 2]
                    act(s_, xT[:, c, tsl], AF.Square, [("xT", c, tc)], [("sq", n % 2)])
                    mm(pb[:, :], ones_b[:, :], s_, c == 0, c == 7, [("sq", n % 2), "ones"], [("ps", ci % 2)], sig=True)
                    n += 1
                r_ = rstd[ci % 2]
                act(r_, pb[:, :], AF.Sqrt, [("ps", ci % 2), "eps"], [("rstd", ci % 2)], scale=1.0 / D, bias=epsT[:, 0:1])
                S.op("dve", lambda e, r_=r_: e.reciprocal(out=r_, in_=r_), [("rstd", ci % 2)], [("rstd", ci % 2)])
                for c in range(8):
                    t_ = tmp[c % 2]
                    tt(t_, xT[:, c, tsl], r_, ALU.mult, [("xT", c, tc), ("rstd", ci % 2)], [("tmp", c % 2)])
                    if out_fn is None:
                        act(hT[:, c, tsl], t_, AF.Identity, [("tmp", c % 2), Akey, Bkey], [("hT", c, tc)],
                            scale=A_ap[:, c:c + 1], bias=B_ap[:, c:c + 1])
                    else:
                        out_fn(c, tc, t_, ("tmp", c % 2))

        def ffn(l, jf, half):
            jn = 0 if jf == 0 else 2
            chunks = [2 * half, 2 * half + 1]
            o = 0
            aT = carve(o, [NFT, 1024], BF16); o += NFT * 1024 * 2
            wgr = [carve(o + i * 2048, [8, 128], BF16) for i in range(3)]; o += 3 * 2048
            wur = [carve(o + i * 2048, [8, 128], BF16) for i in range(3)]; o += 3 * 2048
            wdr = [carve(o + i * NFT * 256, [NFT, 128], BF16) for i in range(2)]; o += 2 * NFT * 256
            sg = [carve(o + i * 2048, [1, 512], F32)[:, 0, :] for i in range(2)]; o += 4096
            o_sq = o; o += 2048
            o_rstd = o; o += 4096
            o_tmp = o; o += 4096
            norm_mod(jn, chunks, modA[:, jn * 8:(jn + 1) * 8], modT[:, (3 * jn) * 8:(3 * jn + 1) * 8],
                     ("modA", jn), "modT", o_sq, o_rstd, o_tmp)
            base = (l * 2 + jf) * NFT
            n = 0
            for ft in range(NFT):
                r = ft % 3
                dma("pool", wgr[r][:, :, :], dr["wg"][base + ft].rearrange("p (c f) -> p c f", c=8), (), [("wg", r)])
                dma("pool", wur[r][:, :, :], dr["wu"][base + ft].rearrange("p (c f) -> p c f", c=8), (), [("wu", r)])
                bset = 4 * (ft % 2)
                for (wr, wk, boff) in ((wgr, "wg", 0), (wur, "wu", 2)):
                    for c in range(8):
                        for ci, tc in enumerate(chunks):
                            mm(ps[bset + boff + ci][:, :], wr[r][:, c, :], hT[:, c, tc * 512:(tc + 1) * 512], c == 0, c == 7,
                               [(wk, r), ("hT", c, tc)], [("ps", bset + boff + ci)], sig=(c == 7))
                for ci, tc in enumerate(chunks):
                    s_ = sg[n % 2]
                    act(s_, ps[bset + ci][:, :], AF.Silu, [("ps", bset + ci)], [("sg", n % 2)])
                    tt(aT[:, ft, ci * 512:(ci + 1) * 512], s_, ps[bset + 2 + ci][:, :], ALU.mult,
                       [("sg", n % 2), ("ps", bset + 2 + ci)], [("aT", ft, ci)])
                    n += 1
            dbase = (l * 2 + jf) * 8
            for dt in range(8):
                r = dt % 2
                dma("pool", wdr[r][:, :, :], dr["wd"][dbase + dt].rearrange("p (f d) -> p f d", f=NFT), (), [("wd", r)])
                for ci, tc in enumerate(chunks):
                    pb = 2 * (dt % 2) + ci
                    for ft in range(NFT):
                        mm(ps[pb][:, :], wdr[r][:, ft, :], aT[:, ft, ci * 512:(ci + 1) * 512], ft == 0, ft == NFT - 1,
                           [("wd", r), ("aT", ft, ci)], [("ps", pb)], sig=(ft == NFT - 1))
                    xs = xT[:, dt, tc * 512:(tc + 1) * 512]
                    stt(xs, ps[pb][:, :], modG[:, jn * 8 + dt: jn * 8 + dt + 1], xs, ALU.mult, ALU.add,
                        [("ps", pb), ("modG", jn), ("xT", dt, tc)], [("xT", dt, tc)])

        tri01 = sb("tri01", [128, 128], F32)
        dma("sp", tri01[:], dr["tri01"][:, :], (), ["tri01"])
        neglam = sb("neglam", [128, 1], F32)
        subgs = sb("subgs", [128, 64], F32)
        lamw = sb("lamw", [128, 128], F32)
        lams = sb("lams", [128, 4], F32)
        tinyT = sb("tinyT", [128, 1], F32)

        def psb(i):
            return ps[i][:, :].bitcast(BF16)

        def outproj_partial(l, yT_sub, nch, c0, wo, key_y):
            for c in range(nch):
                dma("pool", wo[:, c, :], dr["wout"][l * 128:(l + 1) * 128, c0 + c, :], (), [("wo", c)])
            n = 0
            for dt in range(8):
                for tc in range(4):
                    pb = n % 2
                    n += 1
                    for c in range(nch):
                        mm(ps[pb][:, :], wo[:, c, dt * 128:(dt + 1) * 128], yT_sub[:, c, tc * 512:(tc + 1) * 512],
                           c == 0, c == nch - 1, [("wo", c), key_y], [("ps", pb)], sig=(c == nch - 1))
                    xs = xT[:, dt, tc * 512:(tc + 1) * 512]
                    stt(xs, ps[pb][:, :], modG[:, 8 + dt: 9 + dt], xs, ALU.mult, ALU.add,
                        [("ps", pb), ("modG", 1), ("xT", dt, tc)], [("xT", dt, tc)])

        identF = sb("identF", [128, 128], F32)
        cp(identF[:], tri[:, 2, :], ["tri"], ["identF"])

        def attn_stream(sid, Qc, klist, kfn, qfn, vfn, scale, PT, sbanks, otbank, bias_fn, nrows=128, extra=()):
            n = len(klist)

            def emit_S(idx):
                kt, lo, hi = klist[idx]
                b = sbanks[idx % 2]
                c0, c1 = lo * 128, (hi + 1) * 128
                biases = bias_fn(kt, lo, hi)
                mm(ps[b][0:nrows, c0:c1], kfn(kt), qfn(Qc * 512 + c0, Qc * 512 + c1), True, len(biases) == 0,
                   ["K", "Q"] + list(extra), [("ps", b)], sig=(len(biases) == 0))
                for bi, (q, lhs, rhs, rk) in enumerate(biases):
                    mm(ps[b][0:nrows, q * 128:(q + 1) * 128], lhs, rhs, False, True, ["tri", rk], [("ps", b)],
                       sig=(bi == len(biases) - 1))

            emit_S(0)
            yield
            for idx, (kt, lo, hi) in enumerate(klist):
                if idx + 1 < n:
                    emit_S(idx + 1)
                b = sbanks[idx % 2]
                p = idx % 3
                c0, c1 = lo * 128, (hi + 1) * 128
                act(PT[p][0:nrows, c0:c1], ps[b][0:nrows, c0:c1], AF.Exp, [("ps", b)], [("PT", sid, p)], scale=scale)
                yield
                mm(ps[otbank][0:65, c0:c1], vfn(kt), PT[p][0:nrows, c0:c1], idx == 0, idx == n - 1,
                   [("PT", sid, p), "V"], [("ps", otbank)], sig=True)
                yield

        def run_streams(gens):
            live = list(gens)
            while live:
                for g_ in list(live):
                    try:
                        next(g_)
                    except StopIteration:
                        live.remove(g_)

        def finish_stream(otbank, OTs, okey, tbank):
            cp(OTs[0:65, :], ps[otbank][0:65, :], [("ps", otbank)], [okey])
            Otok = ps[tbank][:, 0:260].rearrange("p (q e) -> p q e", e=65)
            for q in range(4):
                S.op("pe", lambda e, q=q: e.transpose(Otok[:, q, :], OTs[0:65, q * 128:(q + 1) * 128], identF[0:65, 0:65]),
                     [okey, "identF"], [("ps", tbank)], sig=(q == 3))
            return Otok

        def causal_klist(Qc):
            return [(kt, max(0, kt - 4 * Qc), 3) for kt in range(4 * Qc + 4)]

        def causal_bias(Qc):
            def f(kt, lo, hi):
                if kt >= 4 * Qc:
                    return [(kt - 4 * Qc, tri[:, 2, :], tri[:, 0, :], "tri")]
                return []
            return f

        def mix_A(l):
            lam_init = 0.8 - 0.6 * float(np.exp(-0.3 * l))
            o = 0
            qT = carve(o, [3, 2048], BF16); o += 12288
            kT = carve(o, [3, 2048], BF16); o += 12288
            vaug = carve(o, [16, 260], BF16); o += 8320
            ya = carve(o, [16, 256], BF16); o += 8192
            o_live = o
            wtok = carve(o, [8, 768], BF16); o += 12288
            wfeat = carve(o, [8, 1024], BF16); o += 16384
            rope = carve(o, [2, 2048], F32); o += 16384
            wsraw = carve(o, [4, 128], F32); o += 2048
            wsT = carve(o, [4, 128], BF16); o += 1024
            lng = carve(o, [1, 256], F32)[:, 0, :]; o += 1024
            bsT = carve(o, [1, 4], F32)[:, 0, :]; o += 16
            guv = [carve(o + i * 2048, [1, 512], F32)[:, 0, :] for i in range(2)]; o += 4096
            vtmp = carve(o, [4, 64], F32); o += 1024
            vn = [carve(o + i * 512, [1, 256], BF16)[:, 0, :] for i in range(2)]; o += 1024
            st = carve(o, [1, 32], F32)[:, 0, :]; o += 128
            rtmp = [carve(o_live + i * 2048, [1, 512], F32)[:, 0, :] for i in range(4)]
            assert o <= SCRB, o

            for c in range(8):
                dma("pool", wtok[:, c, :], dr["wA_tok"][l * 128:(l + 1) * 128, c, :], (), ["wtok"])
                dma("pool", wfeat[:, c, :], dr["wA_feat"][l * 128:(l + 1) * 128, c, :], (), ["wfeat"])
            for i in range(2):
                dma("sp", rope[:, i, :], dr["ropeD"][:, i, :], (), ["rope"])
            dma("sp", wsraw[:, :, :], dr["gmws"][l * 128:(l + 1) * 128, :, :], (), ["wsraw"])
            dma("sp", lng, dr["gmlng"][:, l * 256:(l + 1) * 256], (), ["lng"])
            dma("sp", bsT, dr["gmbs"][l * 128:(l + 1) * 128, :], (), ["bsT"])
            dma("sp", lamw[:], dr["dalam"][:, l * 128:(l + 1) * 128], (), ["lamw"])
            dma("sp", subgs[:], dr["dasubg"][:, l * 64:(l + 1) * 64], (), ["subgs"])
            tt(wsT[:, :, :], wsraw[:, :, :], tri01[:].unsqueeze(1).to_broadcast([128, 4, 128]), ALU.mult,
               ["wsraw", "tri01"], ["wsT"])
            memset(vaug[:, :, :].rearrange("p t (h e) -> p t h e", e=65)[:, :, :, 64:65], 1.0, ["vaug"])
            memset(tinyT[:], 1e-30, ["tiny"])
            tt(lamw[:, 0:32], lamw[:, 0:32], lamw[:, 32:64], ALU.mult, ["lamw"], ["lamw"])
            tt(lamw[:, 64:96], lamw[:, 64:96], lamw[:, 96:128], ALU.mult, ["lamw"], ["lamw"])
            S.op("dve", lambda e: e.tensor_reduce(out=lams[:, 0:1], in_=lamw[:, 0:32], axis=AX.X, op=ALU.add), ["lamw"], ["lams"])
            S.op("dve", lambda e: e.tensor_reduce(out=lams[:, 1:2], in_=lamw[:, 64:96], axis=AX.X, op=ALU.add), ["lamw"], ["lams"])
            act(lams[:, 2:4], lams[:, 0:2], AF.Exp, ["lams"], ["lams2"])
            stt(neglam[:], lams[:, 3:4], -lam_init, lams[:, 2:3], ALU.add, ALU.subtract, ["lams2"], ["neglam"])
            ts(subgs[:], subgs[:], 1.0 - lam_init, None, ALU.mult, None, ["subgs"], ["subgs"])

            def projA(t):
                tsl = slice(t * 128, (t + 1) * 128)
                pU, pV = ps[t % 2], ps[2 + t % 2]
                for c in range(8):
                    mm(pU[:, :], hT[:, c, tsl], wtok[:, c, 0:512], c == 0, c == 7, [("hT", c, t // 4), "wtok"], [("ps", t % 2)], sig=(c == 7))
                for c in range(8):
                    mm(pV[:, 0:256], hT[:, c, tsl], wtok[:, c, 512:768], c == 0, c == 7, [("hT", c, t // 4), "wtok"], [("ps", 2 + t % 2)], sig=(c == 7))

            projA(0)
            for t in range(16):
                if t + 1 < 16:
                    projA(t + 1)
                pU, pV, pS = ps[t % 2], ps[2 + t % 2], ps[4 + t % 2]
                cp(vaug[:, t, :].rearrange("p (h e) -> p h e", e=65)[:, :, 0:64], pV[:, 0:256].rearrange("p (h d) -> p h d", d=64),
                   [("ps", 2 + t % 2)], ["vaug"])
                g_ = guv[t % 2]
                gk = ("guv", t % 2)
                act(g_, pU[:, :], AF.Gelu, [("ps", t % 2)], [gk])
                gv3 = g_[:, 256:512].rearrange("p (g d) -> p g d", d=64)
                S.op("dve", lambda e: e.tensor_reduce(out=st[:, 0:4], in_=gv3, axis=AX.X, op=ALU.add), [gk], ["st0"])
                tt(vtmp[:, :, :], gv3, gv3, ALU.mult, [gk], ["vtmp"])
                S.op("dve", lambda e: e.tensor_reduce(out=st[:, 4:8], in_=vtmp[:, :, :], axis=AX.X, op=ALU.add), ["vtmp"], ["st1"])
                ts(st[:, 8:12], st[:, 0:4], 1.0 / 64, None, ALU.mult, None, ["st0"], ["st2"])
                tt(st[:, 12:16], st[:, 8:12], st[:, 8:12], ALU.mult, ["st2"], ["st3"])
                stt(st[:, 16:20], st[:, 4:8], 1.0 / 64, st[:, 12:16], ALU.mult, ALU.subtract, ["st1", "st3"], ["st4"])
                act(st[:, 20:24], st[:, 16:20], AF.Sqrt, ["st4", "eps"], ["st5"], bias=epsT[:, 0:1])
                S.op("dve", lambda e: e.reciprocal(out=st[:, 24:28], in_=st[:, 20:24]), ["st5"], ["st6"])
                tt(vtmp[:, :, :], gv3, st[:, 8:12].unsqueeze(2).to_broadcast([128, 4, 64]), ALU.subtract, [gk, "st2"], ["vtmp"])
                tt(vtmp[:, :, :], vtmp[:, :, :], st[:, 24:28].unsqueeze(2).to_broadcast([128, 4, 64]), ALU.mult, ["vtmp", "st6"], ["vtmp"])
                v_ = vn[t % 2]
                tt(v_, vtmp[:, :, :].rearrange("p g d -> p (g d)"), lng, ALU.mult, ["vtmp", "lng"], [("vn", t % 2)])
                for g in range(4):
                    mm(pS[:, g * 64:(g + 1) * 64], wsT[:, g, :], v_[:, g * 64:(g + 1) * 64], g == 0, True,
                       ["wsT", ("vn", t % 2)], [("ps", 4 + t % 2)], sig=(g == 3))
                for g in range(4):
                    stt(ya[:, t, g * 64:(g + 1) * 64], pS[:, g * 64:(g + 1) * 64], bsT[:, g:g + 1], g_[:, g * 64:(g + 1) * 64],
                        ALU.add, ALU.mult, [("ps", 4 + t % 2), "bsT", gk], ["ya"])

            S.barrier()
            n = 0
            for which, dst, dkey in ((0, qT, "Q"), (1, kT, "K")):
                for i in range(3):
                    w = 96 if i < 2 else 64
                    cn = which * 512 + i * 96
                    cs = cn + 256
                    for tc in range(4):
                        tsl = slice(tc * 512, (tc + 1) * 512)
                        bn, bs_ = (n % 2) * 2, (n % 2) * 2 + 1
                        for c in range(8):
                            mm(ps[bn][0:w, :], wfeat[:, c, cn:cn + w], hT[:, c, tsl], c == 0, c == 7, ["wfeat", ("hT", c, tc)], [("ps", bn)], sig=(c == 7))
                        for c in range(8):
                            mm(ps[bs_][0:w, :], wfeat[:, c, cs:cs + w], hT[:, c, tsl], c == 0, c == 7, ["wfeat", ("hT", c, tc)], [("ps", bs_)], sig=(c == 7))
                        ta, tb = rtmp[(n % 2) * 2], rtmp[(n % 2) * 2 + 1]
                        tt(ta[0:w, :], ps[bn][0:w, :], rope[0:w, 0, tsl], ALU.mult, [("ps", bn), "rope"], [("rt", (n % 2) * 2)])
                        tt(tb[0:w, :], ps[bs_][0:w, :], rope[0:w, 1, tsl], ALU.mult, [("ps", bs_), "rope"], [("rt", (n % 2) * 2 + 1)])
                        tt(dst[0:w, i, tsl], ta[0:w, :], tb[0:w, :], ALU.add, [("rt", (n % 2) * 2), ("rt", (n % 2) * 2 + 1)], [dkey], eng="pool")
                        n += 1
            S.barrier()

            o = o_live
            PTs = [[carve(o + (m * 3 + i) * 1024, [1, 512], BF16)[:, 0, :] for i in range(3)] for m in range(2)]; o += 6144
            OTs = [carve(o + m * 2048, [1, 512], F32)[:, 0, :] for m in range(2)]; o += 4096
            Oev = [carve(o + i * 1040, [4, 65], F32) for i in range(2)]; o += 2080
            ob = carve(o, [4, 64], F32); o += 1024
            ob2 = carve(o, [4, 64], F32); o += 1024
            sm = carve(o, [1, 32], F32)[:, 0, :]; o += 128
            ystage = carve(o, [4, 256], BF16); o += 2048
            yT_sub = carve(o, [4, 2048], BF16); o += 16384
            wo = carve(o, [4, 1024], BF16); o += 8192
            assert o <= SCRB
            sc_da = 32.0 ** -0.5
            for Qc in range(4):
                for h in range(4):
                    gens = []
                    for m in range(2):
                        p = 2 * h + m
                        i, base = p // 3, 32 * (p % 3)
                        gens.append(attn_stream(m, Qc, causal_klist(Qc),
                                                lambda kt, i=i, base=base: kT[base:base + 32, i, kt * 128:(kt + 1) * 128],
                                                lambda a, b, i=i, base=base: qT[base:base + 32, i, a:b],
                                                lambda kt, h=h: vaug[:, kt, h * 65:(h + 1) * 65],
                                                sc_da, PTs[m], (0, 1) if m == 0 else (6, 7), 2 + m, causal_bias(Qc)))
                    run_streams(gens)
                    for m in range(2):
                        Otok = finish_stream(2 + m, OTs[m], ("OTs", m), 4)
                        cp(Oev[m][:, :, :], Otok, [("ps", 4)], [("Oev", m)], eng="act")
                    S.op("dve", lambda e: e.reciprocal(out=sm[:, 0:4], in_=Oev[0][:, :, 64]), [("Oev", 0)], ["sm0"])
                    S.op("dve", lambda e: e.reciprocal(out=sm[:, 4:8], in_=Oev[1][:, :, 64]), [("Oev", 1)], ["sm1"])
                    ts(sm[:, 4:8], sm[:, 4:8], neglam[:, 0:1], None, ALU.mult, None, ["sm1", "neglam"], ["sm1"])
                    tt(ob[:, :, :], Oev[0][:, :, 0:64], sm[:, 0:4].unsqueeze(2).to_broadcast([128, 4, 64]), ALU.mult, [("Oev", 0), "sm0"], ["ob"])
                    tt(ob2[:, :, :], Oev[1][:, :, 0:64], sm[:, 4:8].unsqueeze(2).to_broadcast([128, 4, 64]), ALU.mult, [("Oev", 1), "sm1"], ["ob2"])
                    tt(ob[:, :, :], ob[:, :, :], ob2[:, :, :], ALU.add, ["ob", "ob2"], ["ob"])
                    tt(ob2[:, :, :], ob[:, :, :], ob[:, :, :], ALU.mult, ["ob"], ["ob2"])
                    S.op("dve", lambda e: e.tensor_reduce(out=sm[:, 8:12], in_=ob2[:, :, :], axis=AX.X, op=ALU.add), ["ob2"], ["sm2"])
                    act(sm[:, 12:16], sm[:, 8:12], AF.Sqrt, ["sm2", "eps"], ["sm3"], scale=1.0 / 64, bias=epsT[:, 0:1])
                    S.op("dve", lambda e: e.reciprocal(out=sm[:, 16:20], in_=sm[:, 12:16]), ["sm3"], ["sm4"])
                    tt(ob[:, :, :], ob[:, :, :], sm[:, 16:20].unsqueeze(2).to_broadcast([128, 4, 64]), ALU.mult, ["ob", "sm4"], ["ob"])
                    tt(ystage[:, :, h * 64:(h + 1) * 64], ob[:, :, :], subgs[:].unsqueeze(1).to_broadcast([128, 4, 64]), ALU.mult,
                       ["ob", "subgs"], ["ystage"])
                for q in range(4):
                    t = Qc * 4 + q
                    bank = 5
                    pb_ = psb(bank)
                    srcs = [ya[:, t, 0:128], ya[:, t, 128:256], ystage[:, q, 0:128], ystage[:, q, 128:256]]
                    for bi, src in enumerate(srcs):
                        S.op("pe", lambda e, src=src, bi=bi: e.transpose(pb_[:, bi * 128:(bi + 1) * 128], src, tri[:, 2, :]),
                             ["ya", "ystage", "tri"], [("ps", bank)], sig=(bi == 3))
                    cp(yT_sub[:, :, t * 128:(t + 1) * 128], pb_[:, 0:512].rearrange("p (c t) -> p c t", c=4), [("ps", bank)], ["yT"], eng="act")
            outproj_partial(l, yT_sub, 4, 0, wo, "yT")

        def mix_N(l, g):
            import os as _os
            o = 0
            qaug = [carve(o + j * 4096, [1, 2048], BF16)[:, 0, :] for j in range(4)]; o += 16384
            kaug = carve(o, [1, 2048], BF16)[:, 0, :]; o += 4096
            kwT = carve(o, [1, 2048], BF16)[:, 0, :]; o += 4096
            vsaug = carve(o, [16, 66], BF16); o += 2112
            vwaug = carve(o, [16, 66], BF16); o += 2112
            gates = carve(o, [16, 12], F32); o += 768
            kcmpT = carve(o, [1, 128], BF16)[:, 0, :]; o += 256
            vcaug = carve(o, [1, 98], BF16)[:, 0, :]; o += 196 + 60
            o_live = o
            wfeat = carve(o, [8, 1024], BF16); o += 16384
            wtok = carve(o, [8, 140], BF16); o += 2240
            RA = o
            rope = carve(o, [2, 2048], F32); o += 16384
            X2 = [carve(o + i * 4096, [1, 2048], BF16)[:, 0, :] for i in range(2)]; o += 8192
            rtmp = [carve(o + i * 2048, [1, 512], F32)[:, 0, :] for i in range(4)]; o += 8192
            Xg = carve(o, [16, 128], BF16); o += 4096
            hs = carve(o, [2, 128], BF16); o += 512
            w2k = carve(o, [2, 128], BF16); o += 512
            w2v = carve(o, [2, 64], BF16); o += 256
            pe2 = carve(o, [2, 16], F32); o += 128
            ropeC = carve(o, [2, 127], F32); o += 1016 + 8
            assert o <= SCRB, o
            w1 = [carve(RA + i * 8192, [16, 256], BF16) for i in range(2)]

            lg = l * 2 + g
            for c in range(8):
                dma("pool", wfeat[:, c, :], dr["wN_feat"][lg * 128:(lg + 1) * 128, c, :], (), ["wfeat"])
            dma("pool", wtok[:, :, :], dr["wN_tok"][lg * 128:(lg + 1) * 128, :, :], (), ["wtok"])
            for i in range(2):
                dma("sp", rope[:, i, :], dr["ropeN"][:, i, :], (), ["RA"])
            dma("pool", kaug[64:96, :], dr["Emat"][:, :], (), ["Kaug_E"])
            memset(vsaug[:, :, 64:65], 1.0, ["vsaug1"])
            memset(vwaug[:, :, 64:65], 1.0, ["vwaug1"])
            memset(vcaug[:, 64:65], 1.0, ["vcaug1"])
            dma("pool", vcaug[:, 65:97], dr["Mov"][:, :], (), ["vcaugM"])
            dma("pool", w2k[:, :, :], dr["cmpw2k"][l * 128:(l + 1) * 128, :, :], (), ["w2k"])
            dma("pool", w2v[:, :, :], dr["cmpw2v"][l * 128:(l + 1) * 128, :, :], (), ["w2v"])
            for kv in range(2):
                dma("sp", pe2[:, kv, :], dr["cmppe"][(l * 2 + kv) * 128:(l * 2 + kv + 1) * 128, :], (), ["pe2"])
            dma("sp", ropeC[0:64, :, :], dr["ropeC"][:, :, :], (), ["ropeC"])

            if _os.environ.get('K_NSTOP') == 'load':
                return
            for t in range(16):
                tsl = slice(t * 128, (t + 1) * 128)
                pT_ = ps[t % 2]
                for c in range(8):
                    mm(pT_[:, 0:140], hT[:, c, tsl], wtok[:, c, :], c == 0, c == 7, [("hT", c, t // 4), "wtok"], [("ps", t % 2)], sig=(c == 7))
                cp(vsaug[:, t, 0:64], pT_[:, 0:64], [("ps", t % 2)], ["vsaug"])
                cp(vwaug[:, t, 0:64], pT_[:, 64:128], [("ps", t % 2)], ["vwaug"])
                act(gates[:, t, :], pT_[:, 128:140], AF.Sigmoid, [("ps", t % 2)], ["gates"])

            if _os.environ.get('K_NSTOP') == 'tok':
                return
            n = 0
            jobs = [(j * 64, 256 + j * 64, qaug[j], ("Q", j)) for j in range(4)] + [(512, 576, kaug, "Kaug"), (640, 704, kwT, "Kw")]
            for cn, cs, dst, dkey in jobs:
                for tc in range(4):
                    tsl = slice(tc * 512, (tc + 1) * 512)
                    bn, bs_ = (n % 2) * 2, (n % 2) * 2 + 1
                    for c in range(8):
                        mm(ps[bn][0:64, :], wfeat[:, c, cn:cn + 64], hT[:, c, tsl], c == 0, c == 7, ["wfeat", ("hT", c, tc)], [("ps", bn)], sig=(c == 7))
                    for c in range(8):
                        mm(ps[bs_][0:64, :], wfeat[:, c, cs:cs + 64], hT[:, c, tsl], c == 0, c == 7, ["wfeat", ("hT", c, tc)], [("ps", bs_)], sig=(c == 7))
                    ta, tb = rtmp[(n % 2) * 2], rtmp[(n % 2) * 2 + 1]
                    tt(ta[0:64, :], ps[bn][0:64, :], rope[0:64, 0, tsl], ALU.mult, [("ps", bn), "RA"], [("rt", (n % 2) * 2)])
                    tt(tb[0:64, :], ps[bs_][0:64, :], rope[0:64, 1, tsl], ALU.mult, [("ps", bs_), "RA"], [("rt", (n % 2) * 2 + 1)])
                    tt(dst[0:64, tsl], ta[0:64, :], tb[0:64, :], ALU.add, [("rt", (n % 2) * 2), ("rt", (n % 2) * 2 + 1)], [dkey], eng="pool")
                    n += 1
            if _os.environ.get('K_NSTOP') == 'feat':
                return
            for kv in range(2):
                cn = 768 + kv * 128
                for tc in range(4):
                    tsl = slice(tc * 512, (tc + 1) * 512)
                    bn = 4 + (n % 2)
                    n += 1
                    for c in range(8):
                        mm(ps[bn][:, :], wfeat[:, c, cn:cn + 128], hT[:, c, tsl], c == 0, c == 7, ["wfeat", ("hT", c, tc)], [("ps", bn)], sig=(c == 7))
                    cp(X2[kv][0:64, tsl], ps[bn][0:64, :], [("ps", bn)], [("X2", kv)])
                    if tc == 0:
                        cp(X2[kv][64:128, 0:511], ps[bn][64:128, 1:512], [("ps", bn)], [("X2", kv)], eng="act")
                    else:
                        cp(X2[kv][64:128, tc * 512 - 1: tc * 512 + 511], ps[bn][64:128, :], [("ps", bn)], [("X2", kv)], eng="act")
            if _os.environ.get('K_NSTOP') == 'x2':
                return
            for kv in range(2):
                for c in range(4):
                    dma("pool", w1[kv][:, c * 4:(c + 1) * 4, :], dr["cmpw1"][(l * 2 + kv) * 128:(l * 2 + kv + 1) * 128, c * 4:(c + 1) * 4, :], (), ["RA"])
            for kv in range(2):
                X3 = X2[kv].rearrange("p (i r) -> p i r", r=16)
                for c in range(16):
                    src = X3[:, 0:127, 2 * c] if c < 8 else X3[:, 1:128, 2 * (c - 8)]
                    ts(Xg[:, c, 0:127], src, pe2[:, kv, c:c + 1], None, ALU.add, None, [("X2", kv), "pe2"], ["Xg"])
                for hh in range(2):
                    for c in range(16):
                        mm(ps[4 + hh][:, 0:127], w1[kv][:, c, hh * 128:(hh + 1) * 128], Xg[:, c, 0:127], c == 0, c == 15, ["RA", "Xg"], [("ps", 4 + hh)], sig=(c == 15))
                    act(hs[:, hh, 0:127], ps[4 + hh][:, 0:127], AF.Silu, [("ps", 4 + hh)], ["hs"])
                if kv == 0:
                    for half_ in range(2):
                        for hh in range(2):
                            mm(ps[6 + half_][0:64, 0:127], w2k[:, hh, half_ * 64:(half_ + 1) * 64], hs[:, hh, 0:127], hh == 0, hh == 1,
                               ["w2k", "hs"], [("ps", 6 + half_)], sig=(hh == 1))
                    tt(rtmp[0][0:64, 0:127], ps[6][0:64, 0:127], ropeC[0:64, 0, :], ALU.mult, [("ps", 6), "ropeC"], [("rt", 0)])
                    tt(rtmp[1][0:64, 0:127], ps[7][0:64, 0:127], ropeC[0:64, 1, :], ALU.mult, [("ps", 7), "ropeC"], [("rt", 1)])
                    tt(kcmpT[0:64, 0:127], rtmp[0][0:64, 0:127], rtmp[1][0:64, 0:127], ALU.add, [("rt", 0), ("rt", 1)], ["Kc"])
                else:
                    for hh in range(2):
                        mm(ps[6][0:127, 0:64], hs[:, hh, 0:127], w2v[:, hh, :], hh == 0, hh == 1, ["w2v", "hs"], [("ps", 6)], sig=(hh == 1))
                    cp(vcaug[0:127, 0:64], ps[6][0:127, 0:64], [("ps", 6)], ["vcaug"])
            S.barrier()

            if _os.environ.get('K_NSTOP') == 'cmp':
                return
            o = o_live
            PTs = [[carve(o + (m * 3 + i) * 1024, [1, 512], BF16)[:, 0, :] for i in range(3)] for m in range(2)]; o += 6144
            PT = PTs[0]
            OTs = [carve(o + m * 2048, [1, 512], F32)[:, 0, :] for m in range(2)]; o += 4096
            acc = [carve(o + j * 1024, [4, 64], F32) for j in range(4)]; o += 4096
            tmpo = carve(o, [4, 64], F32); o += 1024
            imp = carve(o, [4, 32], F32); o += 512
            score = carve(o, [4, 32], F32); o += 512
            work = carve(o, [1, 32], F32)[:, 0, :]; o += 128
            m8 = carve(o, [1, 16], F32)[:, 0, :]; o += 64
            sm = carve(o, [1, 16], F32)[:, 0, :]; o += 64
            negst = carve(o, [4, 128], BF16); o += 1024
            tk = carve(o, [32, 32], F32); o += 4096
            cmpb = carve(o, [1, 2048], BF16)[:, 0, :]; o += 4096
            ystage = carve(o, [4, 256], BF16); o += 2048
            yT_sub = carve(o, [2, 2048], BF16); o += 8192
            wo = carve(o, [2, 1024], BF16); o += 4096
            negT = carve(o, [1, 512], BF16)[:, 0, :]; o += 1024
            assert o <= SCRB
            dma("sp", tk[:, :, :], dr["tk"][:, :, :, :].rearrange("p a q n -> p (a q) n"), (), ["tk"])
            dma("pool", cmpb[:, :], dr["cmpbias"][:, :], (), ["cmpb"])
            memset(negst[:, :, :], 0.0, ["negst"])
            sc_n = 0.125
            nb = 0
            for Qc in range(4):
                q4 = slice(Qc * 4, Qc * 4 + 4)
                for j in range(4):
                    b = nb % 2
                    csl = slice(Qc * 512, (Qc + 1) * 512)
                    mm(ps[b][0:127, :], kcmpT[0:64, 0:127], qaug[j][0:64, csl], True, False, ["Kc", ("Q", j)], [("ps", b)])
                    mm(ps[b][0:127, :], tri[0:127, 2, 0:127], cmpb[0:127, csl], False, True, ["tri", "cmpb"], [("ps", b)], sig=True)
                    p = nb % 3
                    nb += 1
                    act(PT[p][0:127, :], ps[b][0:127, :], AF.Exp, [("ps", b)], [("PT", 0, p)], scale=sc_n)
                    pC = ps[6][:, 0:388].rearrange("p (q e) -> p q e", e=97)
                    for q in range(4):
                        mm(pC[:, q, :], PT[p][0:127, q * 128:(q + 1) * 128], vcaug[0:127, 0:97], q == 0, True,
                           [("PT", 0, p), "vcaug", "vcaug1", "vcaugM"], [("ps", 6)], sig=(q == 3))
                    ts(sm[:, 0:4], pC[:, :, 64], tinyT[:, 0:1], None, ALU.max, None, [("ps", 6), "tiny"], ["sm0"])
                    S.op("dve", lambda e: e.reciprocal(out=sm[:, 4:8], in_=sm[:, 0:4]), ["sm0"], ["sm1"])
                    for q in range(4):
                        if j == 0:
                            ts(imp[:, q, :], pC[:, q, 65:97], sm[:, 4 + q:5 + q], None, ALU.mult, None, [("ps", 6), "sm1"], ["imp"])
                        else:
                            stt(imp[:, q, :], pC[:, q, 65:97], sm[:, 4 + q:5 + q], imp[:, q, :], ALU.mult, ALU.add, [("ps", 6), "sm1", "imp"], ["imp"])
                    tt(sm[:, 8:12], sm[:, 4:8], gates[:, q4, 3 * j], ALU.mult, ["sm1", "gates"], ["sm2"])
                    tt(acc[j][:, :, :], pC[:, :, 0:64], sm[:, 8:12].unsqueeze(2).to_broadcast([128, 4, 64]), ALU.mult, [("ps", 6), "sm2"], [("acc", j)])
                if _os.environ.get('K_ASTOP') == 'ca':
                    return
                tt(score[:, :, :], imp[:, :, :], tk[:, Qc * 4:Qc * 4 + 4, :], ALU.mult, ["imp", "tk"], ["score"])
                tt(score[:, :, :], score[:, :, :], tk[:, 16 + Qc * 4:16 + Qc * 4 + 4, :], ALU.add, ["score", "tk"], ["score"])
                for q in range(4):
                    S.op("dve", lambda e, q=q: e.max(out=m8[:, 0:8], in_=score[:, q, :]), ["score"], ["m8a"])
                    S.op("dve", lambda e, q=q: e.match_replace(out=work[:, :], in_to_replace=m8[:, 0:8], in_values=score[:, q, :], imm_value=-2.0),
                         ["score", "m8a"], ["work"])
                    S.op("dve", lambda e: e.max(out=m8[:, 8:16], in_=work[:, :]), ["work"], ["m8b"])
                    ts(negst[:, q, 64:96], score[:, q, :], m8[:, 15:16], -BIG, ALU.is_lt, ALU.mult, ["score", "m8b"], ["negst"])
                if _os.environ.get('K_ASTOP') == 'tk':
                    return
                pb7 = psb(7)
                for q in range(4):
                    S.op("pe", lambda e, q=q: e.transpose(pb7[:, q * 128:(q + 1) * 128], negst[:, q, :], tri[:, 2, :]),
                         ["negst", "tri"], [("ps", 7)], sig=(q == 3))
                cp(negT[:, :], pb7[:, 0:512], [("ps", 7)], ["negT"], eng="act")
                for j in range(4):
                    dma("sp", qaug[j][64:96, Qc * 512:(Qc + 1) * 512], negT[64:96, :], ["negT"], [("Qn", j)])
                if _os.environ.get('K_ASTOP') == 'nb':
                    return
                kl = []
                for kt in range(max(0, 4 * Qc - 4), 4 * Qc + 4):
                    kl.append((kt, max(kt, 4 * Qc) - 4 * Qc, min(kt + 4, 4 * Qc + 3) - 4 * Qc))

                def wbias(kt, lo, hi, Qc=Qc):
                    r = []
                    if kt >= 4 * Qc:
                        r.append((kt - 4 * Qc, tri[:, 2, :], tri[:, 0, :], "tri"))
                    if kt + 4 <= 4 * Qc + 3:
                        r.append((kt + 4 - 4 * Qc, tri[:, 2, :], tri[:, 1, :], "tri"))
                    return r
                for j in range(4):
                    g1 = attn_stream(0, Qc, causal_klist(Qc),
                                     lambda kt: kaug[0:96, kt * 128:(kt + 1) * 128],
                                     lambda a, b, j=j: qaug[j][0:96, a:b],
                                     lambda kt: vsaug[:, kt, 0:65],
                                     sc_n, PTs[0], (0, 1), 2, causal_bias(Qc), extra=[("Qn", j)])
                    g2 = attn_stream(1, Qc, kl,
                                     lambda kt: kwT[0:64, kt * 128:(kt + 1) * 128],
                                     lambda a, b, j=j: qaug[j][0:64, a:b],
                                     lambda kt: vwaug[:, kt, 0:65],
                                     sc_n, PTs[1], (6, 7), 3, wbias)
                    run_streams([g1, g2])
                    for br in (1, 2):
                        Oacc = finish_stream(1 + br, OTs[br - 1], ("OTs", br - 1), 4)
                        S.op("dve", lambda e: e.reciprocal(out=sm[:, 0:4], in_=Oacc[:, :, 64]), [("ps", 4)], ["sm0"])
                        tt(sm[:, 4:8], sm[:, 0:4], gates[:, q4, 3 * j + br], ALU.mult, ["sm0", "gates"], ["sm1"])
                        tt(tmpo[:, :, :], Oacc[:, :, 0:64], sm[:, 4:8].unsqueeze(2).to_broadcast([128, 4, 64]), ALU.mult, [("ps", 4), "sm1"], ["tmpo"])
                        if br == 1:
                            tt(acc[j][:, :, :], acc[j][:, :, :], tmpo[:, :, :], ALU.add, [("acc", j), "tmpo"], [("acc", j)], eng="pool")
                        else:
                            tt(ystage[:, :, j * 64:(j + 1) * 64], acc[j][:, :, :], tmpo[:, :, :], ALU.add, [("acc", j), "tmpo"], ["ystage"], eng="pool")
                if _os.environ.get('K_ASTOP') == 'att':
                    return
                for q in range(4):
                    t = Qc * 4 + q
                    bank = 5
                    pb_ = psb(bank)
                    for bi in range(2):
                        S.op("pe", lambda e, q=q, bi=bi: e.transpose(pb_[:, bi * 128:(bi + 1) * 128], ystage[:, q, bi * 128:(bi + 1) * 128], tri[:, 2, :]),
                             ["ystage", "tri"], [("ps", bank)], sig=(bi == 1))
                    cp(yT_sub[:, :, t * 128:(t + 1) * 128], pb_[:, 0:256].rearrange("p (c t) -> p c t", c=2), [("ps", bank)], ["yT"], eng="act")
            outproj_partial(l, yT_sub, 2, 4 + 2 * g, wo, "yT")

        for l in range(depth):
            mod_layer(l)
            S.barrier()
            for half in range(2):
                ffn(l, 0, half)
                S.barrier()
            if mix:
                norm_mod(1, [0, 1, 2, 3], modA[:, 8:16], modT[:, 24:32], ("modA", 1), "modT", 0, 4096, 8192)
                S.barrier()
                import os as _os
                if _os.environ.get('K_SKIP_A') is None:
                    mix_A(l)
                    S.barrier()
                for g in range(2):
                    if _os.environ.get('K_SKIP_N') is None:
                        mix_N(l, g)
                        S.barrier()
            for half in range(2):
                ffn(l, 1, half)
                S.barrier()

        if final:
            ostage = [carve(16384 + i * 2048, [1, 512], F32)[:, 0, :] for i in range(4)]
            cnt = [0]

            def out_fn(c, tc, t_, tkey):
                i = cnt[0] % 4
                cnt[0] += 1
                ts(ostage[i], t_, finalg[:, c:c + 1], None, ALU.mult, None, [tkey, "finalg"], [("ost", i)])
                dma("sp", outT_d[c * 128:(c + 1) * 128, tc * 512:(tc + 1) * 512], ostage[i], [("ost", i)], [])
            norm_mod(0, [0, 1, 2, 3], None, None, None, None, 0, 4096, 8192, out_fn=out_fn)
        else:
            for c in range(8):
                for tc in range(4):
                    dma("sp", outT_d[c * 128:(c + 1) * 128, tc * 512:(tc + 1) * 512], xT[:, c, tc * 512:(tc + 1) * 512],
                        [("xT", c, tc)], [])
        S.final_wait()
        print("bass ops:", S.nops)
    return nc


def kernel(_depth=DEPTH, _mix=True, _final=True, **inp):
    inp = {k: np.asarray(v) for k, v in inp.items()}
    shared = _prep_shared(inp, _depth)
    B = inp["x"].shape[0]
    in_maps = []
    for b in range(B):
        m = dict(shared)
        m["xT"] = np.ascontiguousarray(inp["x"][b].T.astype(np.float32))
        m["cT"] = np.ascontiguousarray(inp["c"][b].reshape(8, 128).T.astype(np.float32))
        in_maps.append(m)
    shapes = {k: v.shape for k, v in in_maps[0].items()}
    nc = build(_depth, shapes, mix=_mix, final=_final)
    res = run_bass_kernel_spmd(nc, in_maps, core_ids=list(range(B)))
    out = np.stack([np.ascontiguousarray(res.results[b]["outT"].T) for b in range(B)])
    return out.astype(np.float32)
```

```python
import numpy as np
import ml_dtypes
from contextlib import ExitStack
import concourse.bass as bass
import concourse.mybir as mybir
from concourse.bass_utils import run_bass_kernel_spmd

F32 = mybir.dt.float32
BF16 = mybir.dt.bfloat16
AF = mybir.ActivationFunctionType
ALU = mybir.AluOpType
AX = mybir.AxisListType

S_LEN = 2048
D = 1024
DFF = 2816
NFT = 22
DEPTH = 4
EPS = 1e-6
BIG = 1000.0
D_IN = 2584


class Sched:
    KDMA = 8

    def __init__(self, nc, es):
        self.nc = nc
        self.eng = {"pe": nc.tensor, "act": nc.scalar, "dve": nc.vector, "pool": nc.gpsimd, "sp": nc.sync}
        self.sem = {e: es.enter_context(nc.semaphore("s_" + e)) for e in ("pe", "act", "dve", "pool")}
        self.dsem = {q: [es.enter_context(nc.semaphore("d_%s%d" % (q, i))) for i in range(self.KDMA)]
                     for q in ("sp", "pool", "act")}
        self.dn = {q: 0 for q in self.dsem}
        self.dlast = {q: [0] * self.KDMA for q in self.dsem}
        self.cnt = {e: 0 for e in self.sem}
        self.seen = {e: {} for e in self.eng}
        self.last_w = {}
        self.readers = {}
        self.pe_seq = 0
        self.pe_flags = []
        self.nops = 0

    def _resolve(self, tok):
        if tok[0] == "c":
            return ("c" + tok[1], self.sem[tok[1]], tok[2], tok[1])
        if tok[0] == "pe":
            seq = tok[1]
            best = None
            lo, hi = 0, len(self.pe_flags)
            while lo < hi:
                mid = (lo + hi) // 2
                if self.pe_flags[mid][0] >= seq:
                    hi = mid
                else:
                    lo = mid + 1
            assert lo < len(self.pe_flags), "dependency on unflagged trailing PE op"
            best = self.pe_flags[lo][1]
            return ("cpe", self.sem["pe"], best, "pe")
        q, idx, val = tok[1], tok[2], tok[3]
        return ("d%s%d" % (q, idx), self.dsem[q][idx], val, "dma")

    def _wait(self, eng, need):
        E = self.eng[eng]
        for key, (sem, val) in need.items():
            if self.seen[eng].get(key, 0) < val:
                E.wait_ge(sem, val)
                self.seen[eng][key] = val

    def op(self, eng, fn, reads=(), writes=(), sig=True, dma=False):
        deps = []
        for r in reads:
            t = self.last_w.get(r)
            if t is not None:
                deps.append((t, True))
        for w in writes:
            t = self.last_w.get(w)
            if t is not None:
                deps.append((t, False))
            for t in self.readers.get(w, ()):
                deps.append((t, False))
        need = {}
        for tok, raw in deps:
            if tok[0] == "pe" and eng == "pe" and not dma:
                continue
            if tok[0] == "c" and tok[1] == eng and not raw and not dma:
                continue
            key, sem, val, _ = self._resolve(tok)
            if key not in need or need[key][1] < val:
                need[key] = (sem, val)
        self._wait(eng, need)
        ins = fn(self.eng[eng])
        self.nops += 1
        if dma:
            q = eng
            k = self.dn[q]
            idx = k % self.KDMA
            val = 16 * (k // self.KDMA + 1)
            ins.then_inc(self.dsem[q][idx], 16)
            self.dn[q] += 1
            self.dlast[q][idx] = val
            tok = ("d", q, idx, val)
        elif eng == "pe":
            self.pe_seq += 1
            if sig:
                self.cnt["pe"] += 1
                ins.then_inc(self.sem["pe"], 1)
                self.pe_flags.append((self.pe_seq, self.cnt["pe"]))
            tok = ("pe", self.pe_seq)
        else:
            self.cnt[eng] += 1
            ins.then_inc(self.sem[eng], 1)
            tok = ("c", eng, self.cnt[eng])
        for r in reads:
            self.readers.setdefault(r, []).append(tok)
        for w in writes:
            self.last_w[w] = tok
            self.readers[w] = []
        return tok

    def barrier(self, engines=("pe", "act", "dve", "pool", "sp")):
        assert not self.pe_flags or self.pe_flags[-1][0] == self.pe_seq, "last PE op must be flagged before barrier"
        for e in engines:
            need = {}
            for c in ("pe", "act", "dve", "pool"):
                if self.cnt[c] > 0 and c != e:
                    need["c" + c] = (self.sem[c], self.cnt[c])
            for q in self.dsem:
                for i in range(self.KDMA):
                    if self.dlast[q][i] > 0:
                        need["d%s%d" % (q, i)] = (self.dsem[q][i], self.dlast[q][i])
            self._wait(e, need)
        self.last_w.clear()
        self.readers.clear()

    def final_wait(self):
        need = {}
        for c in ("pe", "act", "dve", "pool"):
            if self.cnt[c] > 0:
                need["c" + c] = (self.sem[c], self.cnt[c])
        for q in self.dsem:
            for i in range(self.KDMA):
                if self.dlast[q][i] > 0:
                    need["d%s%d" % (q, i)] = (self.dsem[q][i], self.dlast[q][i])
        self._wait("sp", need)


def _rope_tables(half, rows, positions):
    inv = (np.float32(10000.0) ** (-(np.arange(half, dtype=np.float32)) / np.float32(half))).astype(np.float32)
    ang = (positions.astype(np.float32)[None, :] * inv[:, None]).astype(np.float32)
    cos = np.cos(ang.astype(np.float64)).astype(np.float32)
    sin = np.sin(ang.astype(np.float64)).astype(np.float32)
    r = np.arange(rows)
    i = r % half
    sign = np.where((r % (2 * half)) < half, -1.0, 1.0).astype(np.float32)
    return np.ascontiguousarray(cos[i]), np.ascontiguousarray(sin[i] * sign[:, None])


def _consts():
    c = {}
    pos = np.arange(S_LEN)
    cd, sd = _rope_tables(16, 128, pos)
    cn, sn = _rope_tables(32, 128, pos)
    c["ropeD"] = np.stack([cd, sd], 1)
    c["ropeN"] = np.stack([cn, sn], 1)
    cend = np.arange(127) * 16 + 31
    cc, sc = _rope_tables(32, 64, cend)
    c["ropeC"] = np.ascontiguousarray(np.stack([cc, sc], 1)).astype(np.float32)
    k = np.arange(128)[:, None]
    t = np.arange(128)[None, :]
    bf = ml_dtypes.bfloat16
    tri = np.zeros((128, 3, 128), np.float32)
    tri[:, 0, :] = np.where(k <= t, 0.0, -BIG)
    tri[:, 1, :] = np.where(k > t, 0.0, -BIG)
    tri[:, 2, :] = np.eye(128)
    c["tri"] = tri
    c["tri01"] = np.where(k <= t, 1.0, 0.0).astype(np.float32)
    cb = np.full((128, S_LEN), -BIG, np.float32)
    cb[:127] = np.where(cend[:, None] <= pos[None, :], 0.0, -BIG)
    c["cmpbias"] = cb
    E = np.zeros((32, S_LEN), np.float32)
    E[pos // 64, pos] = 1.0
    c["Emat"] = E
    cs = np.arange(127) * 16
    ce = cs + 32
    bs = np.arange(32) * 64
    be = bs + 64
    ov = np.clip(np.minimum(ce[:, None], be[None, :]) - np.maximum(cs[:, None], bs[None, :]), 0, None)
    M = np.zeros((128, 32), np.float32)
    M[:127] = (ov / 16).astype(np.float32)
    c["Mov"] = M
    tt = np.arange(S_LEN)
    cur = (tt // 64)[:, None]
    blk = np.arange(32)[None, :]
    valid = blk <= cur
    forced = valid & ((blk == 0) | (blk >= cur - 1))
    m01 = (valid & ~forced).astype(np.float32)
    add = np.where(forced, 1000.0, np.where(valid, 0.0, -1.0)).astype(np.float32)
    c["tk"] = np.ascontiguousarray(
        np.stack([m01.reshape(16, 128, 32).transpose(1, 0, 2), add.reshape(16, 128, 32).transpose(1, 0, 2)], 1))
    return c


def _win_cols():
    o = {}
    offs = np.cumsum([0, 256, 256, 256, 256, 256, 512, 128, 128, 128, 128, 128, 128, 24])
    names = ["gu", "gv", "dq", "dk", "dv", "nq", "nkc", "nvc", "nks", "nvs", "nkw", "nvw", "ng"]
    off = {n: int(offs[i]) for i, n in enumerate(names)}
    sw32 = (np.arange(32) + 16) % 32
    sw64 = (np.arange(64) + 32) % 64
    A_tok = np.concatenate([off["gu"] + np.arange(256), off["gv"] + np.arange(256), off["dv"] + np.arange(256)])
    dq = off["dq"] + np.arange(256)
    dk = off["dk"] + np.arange(256)
    dq_s = off["dq"] + (np.arange(8)[:, None] * 32 + sw32[None, :]).reshape(-1)
    dk_s = off["dk"] + (np.arange(8)[:, None] * 32 + sw32[None, :]).reshape(-1)
    A_feat = np.concatenate([dq, dq_s, dk, dk_s])
    o["A_tok"], o["A_feat"] = A_tok, A_feat
    for g in range(2):
        q = off["nq"] + g * 256 + np.arange(256)
        q_s = off["nq"] + g * 256 + (np.arange(4)[:, None] * 64 + sw64[None, :]).reshape(-1)
        ks = off["nks"] + g * 64 + np.arange(64)
        ks_s = off["nks"] + g * 64 + sw64
        kw = off["nkw"] + g * 64 + np.arange(64)
        kw_s = off["nkw"] + g * 64 + sw64
        kc = off["nkc"] + g * 64 + np.arange(64)
        vc = off["nvc"] + g * 64 + np.arange(64)
        o["N_feat%d" % g] = np.concatenate([q, q_s, ks, ks_s, kw, kw_s, kc, kc, vc, vc])
        vs = off["nvs"] + g * 64 + np.arange(64)
        vw = off["nvw"] + g * 64 + np.arange(64)
        gt = off["ng"] + g * 12 + np.arange(12)
        o["N_tok%d" % g] = np.concatenate([vs, vw, gt])
    return o


def _pk(w, ncols):
    return np.ascontiguousarray(w.reshape(8, 128, ncols).transpose(1, 0, 2))


def _prep_shared(inp, depth):
    f = np.float32
    L = depth
    sh = {}
    cols = _win_cols()
    wg = inp["ffn_w_gate"][:L].reshape(L, 2, 8, 128, NFT, 128).transpose(0, 1, 4, 3, 2, 5)
    sh["wg"] = np.ascontiguousarray(wg).reshape(L * 2 * NFT, 128, 1024)
    wu = inp["ffn_w_up"][:L].reshape(L, 2, 8, 128, NFT, 128).transpose(0, 1, 4, 3, 2, 5)
    sh["wu"] = np.ascontiguousarray(wu).reshape(L * 2 * NFT, 128, 1024)
    wd = inp["ffn_w_down"][:L].reshape(L, 2, NFT, 128, 8, 128).transpose(0, 1, 4, 3, 2, 5)
    sh["wd"] = np.ascontiguousarray(wd).reshape(L * 2 * 8, 128, NFT * 128)
    sh["wada"] = np.ascontiguousarray(inp["w_ada"][:L]).reshape(L * 1024, 9216)
    sh["bada"] = np.ascontiguousarray(inp["b_ada"][:L].reshape(L, 72, 128).transpose(2, 0, 1)).reshape(128, L * 72)
    sh["normg"] = np.ascontiguousarray(inp["norm_g"][:L].reshape(L, 24, 128).transpose(2, 0, 1)).reshape(128, L * 24)
    sh["finalg"] = np.ascontiguousarray(inp["final_g"].reshape(8, 128).T)
    win = inp["w_in"][:L]
    sh["wA_tok"] = np.stack([_pk(win[l][:, cols["A_tok"]], 768) for l in range(L)]).reshape(L * 128, 8, 768)
    sh["wA_feat"] = np.stack([_pk(win[l][:, cols["A_feat"]], 1024) for l in range(L)]).reshape(L * 128, 8, 1024)
    sh["wN_feat"] = np.stack([_pk(win[l][:, cols["N_feat%d" % g]], 1024) for l in range(L) for g in range(2)]).reshape(L * 2 * 128, 8, 1024)
    sh["wN_tok"] = np.stack([_pk(win[l][:, cols["N_tok%d" % g]], 140) for l in range(L) for g in range(2)]).reshape(L * 2 * 128, 8, 140)
    sh["wout"] = np.stack([_pk(inp["w_out"][l], 1024) for l in range(L)]).reshape(L * 128, 8, 1024)
    sh["gmlng"] = np.ascontiguousarray(np.broadcast_to(inp["gm_ln_g"][:L].reshape(1, L * 256), (128, L * 256)))
    sh["gmws"] = np.ascontiguousarray(inp["gm_w_s"][:L].transpose(0, 3, 1, 2)).reshape(L * 128, 4, 128)
    sh["gmbs"] = np.ascontiguousarray(inp["gm_b_s"][:L].transpose(0, 2, 1)).reshape(L * 128, 4)
    sh["dalam"] = np.ascontiguousarray(np.broadcast_to(inp["da_lambda"][:L].reshape(1, L * 128), (128, L * 128)))
    sh["dasubg"] = np.ascontiguousarray(np.broadcast_to(inp["da_sub_g"][:L].reshape(1, L * 64), (128, L * 64)))
    pe = inp["nsa_cmp_pe"][:L].reshape(L, 2, 16, 2, 64).transpose(0, 1, 3, 4, 2)
    sh["cmppe"] = np.ascontiguousarray(pe).reshape(L * 2 * 128, 16)
    w1 = inp["nsa_cmp_w1"][:L].reshape(L, 2, 16, 128, 256).transpose(0, 1, 3, 2, 4)
    sh["cmpw1"] = np.ascontiguousarray(w1).reshape(L * 2 * 128, 16, 256)
    w2 = inp["nsa_cmp_w2"][:L].reshape(L, 2, 2, 128, 64).transpose(0, 1, 3, 2, 4)
    sw64 = (np.arange(64) + 32) % 64
    w2k = np.concatenate([w2[:, 0], w2[:, 0][..., sw64]], -1)
    sh["cmpw2k"] = np.ascontiguousarray(w2k).reshape(L * 128, 2, 128)
    sh["cmpw2v"] = np.ascontiguousarray(w2[:, 1]).reshape(L * 128, 2, 64)
    for k, v in _consts().items():
        sh[k] = v
    return {k: np.ascontiguousarray(v, dtype=f) for k, v in sh.items()}


def build(depth=DEPTH, shapes=None, mix=True, final=True):
    nc = bass.Bass("TRN2", target_bir_lowering=False)
    dr = {}
    for name, shp in shapes.items():
        dr[name] = nc.dram_tensor(name, list(shp), F32, kind="ExternalInput").ap()
    outT_d = nc.dram_tensor("outT", [D, S_LEN], F32, kind="ExternalOutput").ap()

    with ExitStack() as es:
        S = Sched(nc, es)

        def sb(name, shape, dt):
            return es.enter_context(nc.sbuf_tensor("sb_" + name, list(shape), dt))

        xT = sb("xT", [128, 8, S_LEN], F32)
        hT = sb("hT", [128, 8, S_LEN], BF16)
        SCRB = 100 * 1024
        scr = sb("scr", [128, SCRB // 2], BF16)
        modT = sb("modT", [128, 72], F32)
        modA = sb("modA", [128, 24], F32)
        modG = sb("modG", [128, 24], F32)
        normg = sb("normg", [128, depth * 24], F32)
        bada = sb("bada", [128, depth * 72], F32)
        finalg = sb("finalg", [128, 8], F32)
        cact = sb("cact", [128, 8], F32)
        cactb = sb("cactb", [128, 8], BF16)
        ones_b = sb("ones_b", [128, 128], BF16)
        epsT = sb("epsT", [128, 1], F32)
        tri = sb("tri", [128, 3, 128], BF16)
        ps = [es.enter_context(nc.psum_tensor("ps%d" % i, [128, 512], F32)) for i in range(8)]

        def carve(off, shape, dt):
            n = int(np.prod(shape))
            bpe = 4 if dt == F32 else 2
            assert off % 4 == 0 and off + n * bpe <= SCRB, (off, shape)
            v = scr[:, off // 2: off // 2 + n * bpe // 2]
            if dt == F32:
                v = v.bitcast(F32)
            if len(shape) == 2:
                return v.rearrange("p (a b) -> p a b", a=shape[0])
            if len(shape) == 3:
                return v.rearrange("p (a b c) -> p a b c", a=shape[0], b=shape[1])
            return v

        def mm(out, lhsT, rhs, start, stop, reads, writes, sig=False):
            return S.op("pe", lambda e: e.matmul(out, lhsT=lhsT, rhs=rhs, start=start, stop=stop,
                                                 skip_group_check=True), reads, writes, sig=sig)

        def act(out, in_, func, reads, writes, scale=1.0, bias=None, accum=None):
            kw = {}
            if bias is not None:
                kw["bias"] = bias
            if accum is not None:
                kw["accum_out"] = accum
            return S.op("act", lambda e: e.activation(out=out, in_=in_, func=func, scale=scale, **kw), reads, writes)

        def tt(out, in0, in1, op, reads, writes, eng="dve"):
            return S.op(eng, lambda e: e.tensor_tensor(out=out, in0=in0, in1=in1, op=op), reads, writes)

        def ts(out, in0, s1, s2, op0, op1, reads, writes, eng="dve"):
            if op1 is None:
                return S.op(eng, lambda e: e.tensor_scalar(out=out, in0=in0, scalar1=s1, scalar2=None, op0=op0), reads, writes)
            return S.op(eng, lambda e: e.tensor_scalar(out=out, in0=in0, scalar1=s1, scalar2=s2, op0=op0, op1=op1), reads, writes)

        def stt(out, in0, scalar, in1, op0, op1, reads, writes):
            return S.op("dve", lambda e: e.scalar_tensor_tensor(out=out, in0=in0, scalar=scalar, in1=in1, op0=op0, op1=op1), reads, writes)

        def cp(out, in_, reads, writes, eng="dve"):
            if eng == "act":
                return S.op("act", lambda e: e.activation(out=out, in_=in_, func=AF.Copy), reads, writes)
            return S.op(eng, lambda e: e.tensor_copy(out=out, in_=in_), reads, writes)

        def dma(q, out, in_, reads, writes):
            return S.op(q, lambda e: e.dma_start(out=out, in_=in_), reads, writes, dma=True)

        def memset(ap, val, writes, eng="dve"):
            return S.op(eng, lambda e: e.memset(ap, val), (), writes)

        for c in range(8):
            for h in range(2):
                dma("sp", xT[:, c, h * 1024:(h + 1) * 1024], dr["xT"][c * 128:(c + 1) * 128, h * 1024:(h + 1) * 1024],
                    (), [("xT", c, 2 * h), ("xT", c, 2 * h + 1)])
        dma("sp", cact[:], dr["cT"][:, :], (), ["cact"])
        dma("sp", normg[:], dr["normg"][:, :], (), ["normg"])
        dma("sp", bada[:], dr["bada"][:, :], (), ["bada"])
        dma("sp", finalg[:], dr["finalg"][:, :], (), ["finalg"])
        dma("pool", tri[:], dr["tri"][:, :, :], (), ["tri"])
        memset(ones_b[:], 1.0, ["ones"])
        memset(epsT[:], EPS, ["eps"])
        act(cact[:], cact[:], AF.Silu, ["cact"], ["cact"])
        cp(cactb[:], cact[:], ["cact"], ["cactb"])

        def mod_layer(l):
            wada_ring = [carve(i * 4608, [1, 2304], BF16)[:, 0, :] for i in range(3)]
            nslot = 0
            for kc in range(8):
                for q in range(4):
                    slot = wada_ring[nslot % 3]
                    key = ("wada", nslot % 3)
                    nslot += 1
                    dma("pool", slot[:], dr["wada"][l * 1024 + kc * 128: l * 1024 + (kc + 1) * 128, q * 2304:(q + 1) * 2304],
                        (), [key])
                    for t in range(18):
                        ct = q * 18 + t
                        first = (kc == 0 and ct == 0)
                        last = (kc == 7 and ct == 71)
                        mm(ps[0][:, ct:ct + 1], slot[:, t * 128:(t + 1) * 128], cactb[:, kc:kc + 1], first, kc == 7,
                           [key, "cactb"], [("ps", 0)], sig=(t == 17))
            tt(modT[:], ps[0][:, 0:72], bada[:, l * 72:(l + 1) * 72], ALU.add, [("ps", 0), "bada"], ["modT"])
            for j in range(3):
                stt(modA[:, j * 8:(j + 1) * 8], modT[:, (3 * j + 1) * 8:(3 * j + 2) * 8], 1.0,
                    normg[:, l * 24 + j * 8: l * 24 + (j + 1) * 8], ALU.add, ALU.mult, ["modT", "normg"], [("modA", j)])
                ts(modG[:, j * 8:(j + 1) * 8], modT[:, (3 * j + 2) * 8:(3 * j + 3) * 8], 0.5 if j != 1 else 1.0, None,
                   ALU.mult, None, ["modT"], [("modG", j)])

        def norm_mod(j, chunks, A_ap, B_ap, Akey, Bkey, o_sq, o_rstd, o_tmp, out_fn=None):
            sq = [carve(o_sq + i * 1024, [1, 512], BF16)[:, 0, :] for i in range(2)]
            rstd = [carve(o_rstd + i * 2048, [1, 512], F32)[:, 0, :] for i in range(2)]
            tmp = [carve(o_tmp + i * 2048, [1, 512], F32)[:, 0, :] for i in range(2)]
            n = 0
            for ci, tc in enumerate(chunks):
                tsl = slice(tc * 512, (tc + 1) * 512)
                pb = ps[ci % 2]
                for c in range(8):
                    s_ = sq[n % 2]
                    act(s_, xT[:, c, tsl], AF.Square, [("xT", c, tc)], [("sq", n % 2)])
                    mm(pb[:, :], ones_b[:, :], s_, c == 0, c == 7, [("sq", n % 2), "ones"], [("ps", ci % 2)], sig=True)
                    n += 1
                r_ = rstd[ci % 2]
                act(r_, pb[:, :], AF.Sqrt, [("ps", ci % 2), "eps"], [("rstd", ci % 2)], scale=1.0 / D, bias=epsT[:, 0:1])
                S.op("dve", lambda e, r_=r_: e.reciprocal(out=r_, in_=r_), [("rstd", ci % 2)], [("rstd", ci % 2)])
                for c in range(8):
                    t_ = tmp[c % 2]
                    tt(t_, xT[:, c, tsl], r_, ALU.mult, [("xT", c, tc), ("rstd", ci % 2)], [("tmp", c % 2)])
                    if out_fn is None:
                        act(hT[:, c, tsl], t_, AF.Identity, [("tmp", c % 2), Akey, Bkey], [("hT", c, tc)],
                            scale=A_ap[:, c:c + 1], bias=B_ap[:, c:c + 1])
                    else:
                        out_fn(c, tc, t_, ("tmp", c % 2))

        def ffn(l, jf, half):
            jn = 0 if jf == 0 else 2
            chunks = [2 * half, 2 * half + 1]
            o = 0
            aT = carve(o, [NFT, 1024], BF16); o += NFT * 1024 * 2
            wgr = [carve(o + i * 2048, [8, 128], BF16) for i in range(3)]; o += 3 * 2048
            wur = [carve(o + i * 2048, [8, 128], BF16) for i in range(3)]; o += 3 * 2048
            wdr = [carve(o + i * NFT * 256, [NFT, 128], BF16) for i in range(2)]; o += 2 * NFT * 256
            sg = [carve(o + i * 2048, [1, 512], F32)[:, 0, :] for i in range(2)]; o += 4096
            o_sq = o; o += 2048
            o_rstd = o; o += 4096
            o_tmp = o; o += 4096
            norm_mod(jn, chunks, modA[:, jn * 8:(jn + 1) * 8], modT[:, (3 * jn) * 8:(3 * jn + 1) * 8],
                     ("modA", jn), "modT", o_sq, o_rstd, o_tmp)
            base = (l * 2 + jf) * NFT
            n = 0
            for ft in range(NFT):
                r = ft % 3
                dma("pool", wgr[r][:, :, :], dr["wg"][base + ft].rearrange("p (c f) -> p c f", c=8), (), [("wg", r)])
                dma("pool", wur[r][:, :, :], dr["wu"][base + ft].rearrange("p (c f) -> p c f", c=8), (), [("wu", r)])
                bset = 4 * (ft % 2)
                for (wr, wk, boff) in ((wgr, "wg", 0), (wur, "wu", 2)):
                    for c in range(8):
                        for ci, tc in enumerate(chunks):
                            mm(ps[bset + boff + ci][:, :], wr[r][:, c, :], hT[:, c, tc * 512:(tc + 1) * 512], c == 0, c == 7,
                               [(wk, r), ("hT", c, tc)], [("ps", bset + boff + ci)], sig=(c == 7))
                for ci, tc in enumerate(chunks):
                    s_ = sg[n % 2]
                    act(s_, ps[bset + ci][:, :], AF.Silu, [("ps", bset + ci)], [("sg", n % 2)])
                    tt(aT[:, ft, ci * 512:(ci + 1) * 512], s_, ps[bset + 2 + ci][:, :], ALU.mult,
                       [("sg", n % 2), ("ps", bset + 2 + ci)], [("aT", ft, ci)])
                    n += 1
            dbase = (l * 2 + jf) * 8
            for dt in range(8):
                r = dt % 2
                dma("pool", wdr[r][:, :, :], dr["wd"][dbase + dt].rearrange("p (f d) -> p f d", f=NFT), (), [("wd", r)])
                for ci, tc in enumerate(chunks):
                    pb = 2 * (dt % 2) + ci
                    for ft in range(NFT):
                        mm(ps[pb][:, :], wdr[r][:, ft, :], aT[:, ft, ci * 512:(ci + 1) * 512], ft == 0, ft == NFT - 1,
                           [("wd", r), ("aT", ft, ci)], [("ps", pb)], sig=(ft == NFT - 1))
                    xs = xT[:, dt, tc * 512:(tc + 1) * 512]
                    stt(xs, ps[pb][:, :], modG[:, jn * 8 + dt: jn * 8 + dt + 1], xs, ALU.mult, ALU.add,
                        [("ps", pb), ("modG", jn), ("xT", dt, tc)], [("xT", dt, tc)])

        tri01 = sb("tri01", [128, 128], F32)
        dma("sp", tri01[:], dr["tri01"][:, :], (), ["tri01"])
        neglam = sb("neglam", [128, 1], F32)
        subgs = sb("subgs", [128, 64], F32)
        lamw = sb("lamw", [128, 128], F32)
        lams = sb("lams", [128, 4], F32)
        tinyT = sb("tinyT", [128, 1], F32)

        def psb(i):
            return ps[i][:, :].bitcast(BF16)

        def outproj_partial(l, yT_sub, nch, c0, wo, key_y):
            for c in range(nch):
                dma("pool", wo[:, c, :], dr["wout"][l * 128:(l + 1) * 128, c0 + c, :], (), [("wo", c)])
            n = 0
            for dt in range(8):
                for tc in range(4):
                    pb = n % 2
                    n += 1
                    for c in range(nch):
                        mm(ps[pb][:, :], wo[:, c, dt * 128:(dt + 1) * 128], yT_sub[:, c, tc * 512:(tc + 1) * 512],
                           c == 0, c == nch - 1, [("wo", c), key_y], [("ps", pb)], sig=(c == nch - 1))
                    xs = xT[:, dt, tc * 512:(tc + 1) * 512]
                    stt(xs, ps[pb][:, :], modG[:, 8 + dt: 9 + dt], xs, ALU.mult, ALU.add,
                        [("ps", pb), ("modG", 1), ("xT", dt, tc)], [("xT", dt, tc)])

        identF = sb("identF", [128, 128], F32)
        cp(identF[:], tri[:, 2, :], ["tri"], ["identF"])

        def attn_stream(sid, Qc, klist, kfn, qfn, vfn, scale, PT, sbanks, otbank, bias_fn, nrows=128, extra=()):
            n = len(klist)

            def emit_S(idx):
                kt, lo, hi = klist[idx]
                b = sbanks[idx % 2]
                c0, c1 = lo * 128, (hi + 1) * 128
                biases = bias_fn(kt, lo, hi)
                mm(ps[b][0:nrows, c0:c1], kfn(kt), qfn(Qc * 512 + c0, Qc * 512 + c1), True, len(biases) == 0,
                   ["K", "Q"] + list(extra), [("ps", b)], sig=(len(biases) == 0))
                for bi, (q, lhs, rhs, rk) in enumerate(biases):
                    mm(ps[b][0:nrows, q * 128:(q + 1) * 128], lhs, rhs, False, True, ["tri", rk], [("ps", b)],
                       sig=(bi == len(biases) - 1))

            emit_S(0)
            yield
            for idx, (kt, lo, hi) in enumerate(klist):
                if idx + 1 < n:
                    emit_S(idx + 1)
                b = sbanks[idx % 2]
                p = idx % 3
                c0, c1 = lo * 128, (hi + 1) * 128
                act(PT[p][0:nrows, c0:c1], ps[b][0:nrows, c0:c1], AF.Exp, [("ps", b)], [("PT", sid, p)], scale=scale)
                yield
                mm(ps[otbank][0:65, c0:c1], vfn(kt), PT[p][0:nrows, c0:c1], idx == 0, idx == n - 1,
                   [("PT", sid, p), "V"], [("ps", otbank)], sig=True)
                yield

        def run_streams(gens):
            live = list(gens)
            while live:
                for g_ in list(live):
                    try:
                        next(g_)
                    except StopIteration:
                        live.remove(g_)

        def finish_stream(otbank, OTs, okey, tbank):
            cp(OTs[0:65, :], ps[otbank][0:65, :], [("ps", otbank)], [okey])
            Otok = ps[tbank][:, 0:260].rearrange("p (q e) -> p q e", e=65)
            for q in range(4):
                S.op("pe", lambda e, q=q: e.transpose(Otok[:, q, :], OTs[0:65, q * 128:(q + 1) * 128], identF[0:65, 0:65]),
                     [okey, "identF"], [("ps", tbank)], sig=(q == 3))
            return Otok

        def causal_klist(Qc):
            return [(kt, max(0, kt - 4 * Qc), 3) for kt in range(4 * Qc + 4)]

        def causal_bias(Qc):
            def f(kt, lo, hi):
                if kt >= 4 * Qc:
                    return [(kt - 4 * Qc, tri[:, 2, :], tri[:, 0, :], "tri")]
                return []
            return f

        def mix_A(l):
            lam_init = 0.8 - 0.6 * float(np.exp(-0.3 * l))
            o = 0
            qT = carve(o, [3, 2048], BF16); o += 12288
            kT = carve(o, [3, 2048], BF16); o += 12288
            vaug = carve(o, [16, 260], BF16); o += 8320
            ya = carve(o, [16, 256], BF16); o += 8192
            o_live = o
            wtok = carve(o, [8, 768], BF16); o += 12288
            wfeat = carve(o, [8, 1024], BF16); o += 16384
            rope = carve(o, [2, 2048], F32); o += 16384
            wsraw = carve(o, [4, 128], F32); o += 2048
            wsT = carve(o, [4, 128], BF16); o += 1024
            lng = carve(o, [1, 256], F32)[:, 0, :]; o += 1024
            bsT = carve(o, [1, 4], F32)[:, 0, :]; o += 16
            guv = [carve(o + i * 2048, [1, 512], F32)[:, 0, :] for i in range(2)]; o += 4096
            vtmp = carve(o, [4, 64], F32); o += 1024
            vn = [carve(o + i * 512, [1, 256], BF16)[:, 0, :] for i in range(2)]; o += 1024
            st = carve(o, [1, 32], F32)[:, 0, :]; o += 128
            rtmp = [carve(o_live + i * 2048, [1, 512], F32)[:, 0, :] for i in range(4)]
            assert o <= SCRB, o

            for c in range(8):
                dma("pool", wtok[:, c, :], dr["wA_tok"][l * 128:(l + 1) * 128, c, :], (), ["wtok"])
                dma("pool", wfeat[:, c, :], dr["wA_feat"][l * 128:(l + 1) * 128, c, :], (), ["wfeat"])
            for i in range(2):
                dma("sp", rope[:, i, :], dr["ropeD"][:, i, :], (), ["rope"])
            dma("sp", wsraw[:, :, :], dr["gmws"][l * 128:(l + 1) * 128, :, :], (), ["wsraw"])
            dma("sp", lng, dr["gmlng"][:, l * 256:(l + 1) * 256], (), ["lng"])
            dma("sp", bsT, dr["gmbs"][l * 128:(l + 1) * 128, :], (), ["bsT"])
            dma("sp", lamw[:], dr["dalam"][:, l * 128:(l + 1) * 128], (), ["lamw"])
            dma("sp", subgs[:], dr["dasubg"][:, l * 64:(l + 1) * 64], (), ["subgs"])
            tt(wsT[:, :, :], wsraw[:, :, :], tri01[:].unsqueeze(1).to_broadcast([128, 4, 128]), ALU.mult,
               ["wsraw", "tri01"], ["wsT"])
            memset(vaug[:, :, :].rearrange("p t (h e) -> p t h e", e=65)[:, :, :, 64:65], 1.0, ["vaug"])
            memset(tinyT[:], 1e-30, ["tiny"])
            tt(lamw[:, 0:32], lamw[:, 0:32], lamw[:, 32:64], ALU.mult, ["lamw"], ["lamw"])
            tt(lamw[:, 64:96], lamw[:, 64:96], lamw[:, 96:128], ALU.mult, ["lamw"], ["lamw"])
            S.op("dve", lambda e: e.tensor_reduce(out=lams[:, 0:1], in_=lamw[:, 0:32], axis=AX.X, op=ALU.add), ["lamw"], ["lams"])
            S.op("dve", lambda e: e.tensor_reduce(out=lams[:, 1:2], in_=lamw[:, 64:96], axis=AX.X, op=ALU.add), ["lamw"], ["lams"])
            act(lams[:, 2:4], lams[:, 0:2], AF.Exp, ["lams"], ["lams2"])
            stt(neglam[:], lams[:, 3:4], -lam_init, lams[:, 2:3], ALU.add, ALU.subtract, ["lams2"], ["neglam"])
            ts(subgs[:], subgs[:], 1.0 - lam_init, None, ALU.mult, None, ["subgs"], ["subgs"])

            def projA(t):
                tsl = slice(t * 128, (t + 1) * 128)
                pU, pV = ps[t % 2], ps[2 + t % 2]
                for c in range(8):
                    mm(pU[:, :], hT[:, c, tsl], wtok[:, c, 0:512], c == 0, c == 7, [("hT", c, t // 4), "wtok"], [("ps", t % 2)], sig=(c == 7))
                for c in range(8):
                    mm(pV[:, 0:256], hT[:, c, tsl], wtok[:, c, 512:768], c == 0, c == 7, [("hT", c, t // 4), "wtok"], [("ps", 2 + t % 2)], sig=(c == 7))

            projA(0)
            for t in range(16):
                if t + 1 < 16:
                    projA(t + 1)
                pU, pV, pS = ps[t % 2], ps[2 + t % 2], ps[4 + t % 2]
                cp(vaug[:, t, :].rearrange("p (h e) -> p h e", e=65)[:, :, 0:64], pV[:, 0:256].rearrange("p (h d) -> p h d", d=64),
                   [("ps", 2 + t % 2)], ["vaug"])
                g_ = guv[t % 2]
                gk = ("guv", t % 2)
                act(g_, pU[:, :], AF.Gelu, [("ps", t % 2)], [gk])
                gv3 = g_[:, 256:512].rearrange("p (g d) -> p g d", d=64)
                S.op("dve", lambda e: e.tensor_reduce(out=st[:, 0:4], in_=gv3, axis=AX.X, op=ALU.add), [gk], ["st0"])
                tt(vtmp[:, :, :], gv3, gv3, ALU.mult, [gk], ["vtmp"])
                S.op("dve", lambda e: e.tensor_reduce(out=st[:, 4:8], in_=vtmp[:, :, :], axis=AX.X, op=ALU.add), ["vtmp"], ["st1"])
                ts(st[:, 8:12], st[:, 0:4], 1.0 / 64, None, ALU.mult, None, ["st0"], ["st2"])
                tt(st[:, 12:16], st[:, 8:12], st[:, 8:12], ALU.mult, ["st2"], ["st3"])
                stt(st[:, 16:20], st[:, 4:8], 1.0 / 64, st[:, 12:16], ALU.mult, ALU.subtract, ["st1", "st3"], ["st4"])
                act(st[:, 20:24], st[:, 16:20], AF.Sqrt, ["st4", "eps"], ["st5"], bias=epsT[:, 0:1])
                S.op("dve", lambda e: e.reciprocal(out=st[:, 24:28], in_=st[:, 20:24]), ["st5"], ["st6"])
                tt(vtmp[:, :, :], gv3, st[:, 8:12].unsqueeze(2).to_broadcast([128, 4, 64]), ALU.subtract, [gk, "st2"], ["vtmp"])
                tt(vtmp[:, :, :], vtmp[:, :, :], st[:, 24:28].unsqueeze(2).to_broadcast([128, 4, 64]), ALU.mult, ["vtmp", "st6"], ["vtmp"])
                v_ = vn[t % 2]
                tt(v_, vtmp[:, :, :].rearrange("p g d -> p (g d)"), lng, ALU.mult, ["vtmp", "lng"], [("vn", t % 2)])
                for g in range(4):
                    mm(pS[:, g * 64:(g + 1) * 64], wsT[:, g, :], v_[:, g * 64:(g + 1) * 64], g == 0, True,
                       ["wsT", ("vn", t % 2)], [("ps", 4 + t % 2)], sig=(g == 3))
                for g in range(4):
                    stt(ya[:, t, g * 64:(g + 1) * 64], pS[:, g * 64:(g + 1) * 64], bsT[:, g:g + 1], g_[:, g * 64:(g + 1) * 64],
                        ALU.add, ALU.mult, [("ps", 4 + t % 2), "bsT", gk], ["ya"])

            S.barrier()
            n = 0
            for which, dst, dkey in ((0, qT, "Q"), (1, kT, "K")):
                for i in range(3):
                    w = 96 if i < 2 else 64
                    cn = which * 512 + i * 96
                    cs = cn + 256
                    for tc in range(4):
                        tsl = slice(tc * 512, (tc + 1) * 512)
                        bn, bs_ = (n % 2) * 2, (n % 2) * 2 + 1
                        for c in range(8):
                            mm(ps[bn][0:w, :], wfeat[:, c, cn:cn + w], hT[:, c, tsl], c == 0, c == 7, ["wfeat", ("hT", c, tc)], [("ps", bn)], sig=(c == 7))
                        for c in range(8):
                            mm(ps[bs_][0:w, :], wfeat[:, c, cs:cs + w], hT[:, c, tsl], c == 0, c == 7, ["wfeat", ("hT", c, tc)], [("ps", bs_)], sig=(c == 7))
                        ta, tb = rtmp[(n % 2) * 2], rtmp[(n % 2) * 2 + 1]
                        tt(ta[0:w, :], ps[bn][0:w, :], rope[0:w, 0, tsl], ALU.mult, [("ps", bn), "rope"], [("rt", (n % 2) * 2)])
                        tt(tb[0:w, :], ps[bs_][0:w, :], rope[0:w, 1, tsl], ALU.mult, [("ps", bs_), "rope"], [("rt", (n % 2) * 2 + 1)])
                        tt(dst[0:w, i, tsl], ta[0:w, :], tb[0:w, :], ALU.add, [("rt", (n % 2) * 2), ("rt", (n % 2) * 2 + 1)], [dkey], eng="pool")
                        n += 1
            S.barrier()

            o = o_live
            PTs = [[carve(o + (m * 3 + i) * 1024, [1, 512], BF16)[:, 0, :] for i in range(3)] for m in range(2)]; o += 6144
            OTs = [carve(o + m * 2048, [1, 512], F32)[:, 0, :] for m in range(2)]; o += 4096
            Oev = [carve(o + i * 1040, [4, 65], F32) for i in range(2)]; o += 2080
            obs = [carve(o + i * 1024, [4, 64], F32) for i in range(2)]; o += 2048
            ob2 = carve(o, [4, 64], F32); o += 1024
            sms = [carve(o + i * 128, [1, 32], F32)[:, 0, :] for i in range(2)]; o += 256
            pending = [None]
            ystage = carve(o, [4, 256], BF16); o += 2048
            yT_sub = carve(o, [4, 2048], BF16); o += 16384
            wo = carve(o, [4, 1024], BF16); o += 8192
            assert o <= SCRB
            sc_da = 32.0 ** -0.5
            for Qc in range(4):
                for h in range(4):
                    gens = []
                    for m in range(2):
                        p = 2 * h + m
                        i, base = p // 3, 32 * (p % 3)
                        gens.append(attn_stream(m, Qc, causal_klist(Qc),
                                                lambda kt, i=i, base=base: kT[base:base + 32, i, kt * 128:(kt + 1) * 128],
                                                lambda a, b, i=i, base=base: qT[base:base + 32, i, a:b],
                                                lambda kt, h=h: vaug[:, kt, h * 65:(h + 1) * 65],
                                                sc_da, PTs[m], (0, 1) if m == 0 else (6, 7), 2 + m, causal_bias(Qc)))
                    run_streams(gens)
                    par = h % 2
                    obp, smp = obs[par], sms[par]
                    for m in range(2):
                        Otok = finish_stream(2 + m, OTs[m], ("OTs", m), 4)
                        cp(Oev[m][:, :, :], Otok, [("ps", 4)], [("Oev", m)])
                    S.op("dve", lambda e: e.reciprocal(out=smp[:, 0:4], in_=Oev[0][:, :, 64]), [("Oev", 0)], [("sm0", par)])
                    S.op("dve", lambda e: e.reciprocal(out=smp[:, 4:8], in_=Oev[1][:, :, 64]), [("Oev", 1)], [("sm1", par)])
                    ts(smp[:, 4:8], smp[:, 4:8], neglam[:, 0:1], None, ALU.mult, None, [("sm1", par), "neglam"], [("sm1", par)])
                    tt(obp[:, :, :], Oev[0][:, :, 0:64], smp[:, 0:4].unsqueeze(2).to_broadcast([128, 4, 64]), ALU.mult, [("Oev", 0), ("sm0", par)], [("ob", par)])
                    tt(ob2[:, :, :], Oev[1][:, :, 0:64], smp[:, 4:8].unsqueeze(2).to_broadcast([128, 4, 64]), ALU.mult, [("Oev", 1), ("sm1", par)], ["ob2"])
                    tt(obp[:, :, :], obp[:, :, :], ob2[:, :, :], ALU.add, [("ob", par), "ob2"], [("ob", par)])
                    tt(ob2[:, :, :], obp[:, :, :], obp[:, :, :], ALU.mult, [("ob", par)], ["ob2"])
                    S.op("dve", lambda e: e.tensor_reduce(out=smp[:, 8:12], in_=ob2[:, :, :], axis=AX.X, op=ALU.add), ["ob2"], [("sm2", par)])
                    if pending[0] is not None:
                        pending[0]()

                    def post2(h=h, par=par, obp=obp, smp=smp):
                        act(smp[:, 12:16], smp[:, 8:12], AF.Ln, [("sm2", par), "eps"], [("sm3", par)], scale=1.0 / 64, bias=epsT[:, 0:1])
                        act(smp[:, 16:20], smp[:, 12:16], AF.Exp, [("sm3", par)], [("sm4", par)], scale=-0.5)
                        tt(obp[:, :, :], obp[:, :, :], smp[:, 16:20].unsqueeze(2).to_broadcast([128, 4, 64]), ALU.mult, [("ob", par), ("sm4", par)], [("ob", par)])
                        tt(ystage[:, :, h * 64:(h + 1) * 64], obp[:, :, :], subgs[:].unsqueeze(1).to_broadcast([128, 4, 64]), ALU.mult,
                           [("ob", par), "subgs"], ["ystage"])
                    pending[0] = post2
                pending[0]()
                pending[0] = None
                for q in range(4):
                    t = Qc * 4 + q
                    bank = 5
                    pb_ = psb(bank)
                    srcs = [ya[:, t, 0:128], ya[:, t, 128:256], ystage[:, q, 0:128], ystage[:, q, 128:256]]
                    for bi, src in enumerate(srcs):
                        S.op("pe", lambda e, src=src, bi=bi: e.transpose(pb_[:, bi * 128:(bi + 1) * 128], src, tri[:, 2, :]),
                             ["ya", "ystage", "tri"], [("ps", bank)], sig=(bi == 3))
                    cp(yT_sub[:, :, t * 128:(t + 1) * 128], pb_[:, 0:512].rearrange("p (c t) -> p c t", c=4), [("ps", bank)], ["yT"])
            outproj_partial(l, yT_sub, 4, 0, wo, "yT")

        def mix_N(l, g):
            import os as _os
            o = 0
            qaug = [carve(o + j * 4096, [1, 2048], BF16)[:, 0, :] for j in range(4)]; o += 16384
            kaug = carve(o, [1, 2048], BF16)[:, 0, :]; o += 4096
            kwT = carve(o, [1, 2048], BF16)[:, 0, :]; o += 4096
            vsaug = carve(o, [16, 66], BF16); o += 2112
            vwaug = carve(o, [16, 66], BF16); o += 2112
            gates = carve(o, [16, 12], F32); o += 768
            kcmpT = carve(o, [1, 128], BF16)[:, 0, :]; o += 256
            vcaug = carve(o, [1, 98], BF16)[:, 0, :]; o += 196 + 60
            o_live = o
            wfeat = carve(o, [8, 1024], BF16); o += 16384
            wtok = carve(o, [8, 140], BF16); o += 2240
            RA = o
            rope = carve(o, [2, 2048], F32); o += 16384
            X2 = [carve(o + i * 4096, [1, 2048], BF16)[:, 0, :] for i in range(2)]; o += 8192
            rtmp = [carve(o + i * 2048, [1, 512], F32)[:, 0, :] for i in range(4)]; o += 8192
            Xg = carve(o, [16, 128], BF16); o += 4096
            hs = carve(o, [2, 128], BF16); o += 512
            w2k = carve(o, [2, 128], BF16); o += 512
            w2v = carve(o, [2, 64], BF16); o += 256
            pe2 = carve(o, [2, 16], F32); o += 128
            ropeC = carve(o, [2, 127], F32); o += 1016 + 8
            assert o <= SCRB, o
            w1 = [carve(RA + i * 8192, [16, 256], BF16) for i in range(2)]

            lg = l * 2 + g
            for c in range(8):
                dma("pool", wfeat[:, c, :], dr["wN_feat"][lg * 128:(lg + 1) * 128, c, :], (), ["wfeat"])
            dma("pool", wtok[:, :, :], dr["wN_tok"][lg * 128:(lg + 1) * 128, :, :], (), ["wtok"])
            for i in range(2):
                dma("sp", rope[:, i, :], dr["ropeN"][:, i, :], (), ["RA"])
            dma("pool", kaug[64:96, :], dr["Emat"][:, :], (), ["Kaug_E"])
            memset(vsaug[:, :, 64:65], 1.0, ["vsaug1"])
            memset(vwaug[:, :, 64:65], 1.0, ["vwaug1"])
            memset(vcaug[:, 64:65], 1.0, ["vcaug1"])
            dma("pool", vcaug[:, 65:97], dr["Mov"][:, :], (), ["vcaugM"])
            dma("pool", w2k[:, :, :], dr["cmpw2k"][l * 128:(l + 1) * 128, :, :], (), ["w2k"])
            dma("pool", w2v[:, :, :], dr["cmpw2v"][l * 128:(l + 1) * 128, :, :], (), ["w2v"])
            for kv in range(2):
                dma("sp", pe2[:, kv, :], dr["cmppe"][(l * 2 + kv) * 128:(l * 2 + kv + 1) * 128, :], (), ["pe2"])
            dma("sp", ropeC[0:64, :, :], dr["ropeC"][:, :, :], (), ["ropeC"])

            if _os.environ.get('K_NSTOP') == 'load':
                return
            for t in range(16):
                tsl = slice(t * 128, (t + 1) * 128)
                pT_ = ps[t % 2]
                for c in range(8):
                    mm(pT_[:, 0:140], hT[:, c, tsl], wtok[:, c, :], c == 0, c == 7, [("hT", c, t // 4), "wtok"], [("ps", t % 2)], sig=(c == 7))
                cp(vsaug[:, t, 0:64], pT_[:, 0:64], [("ps", t % 2)], ["vsaug"])
                cp(vwaug[:, t, 0:64], pT_[:, 64:128], [("ps", t % 2)], ["vwaug"])
                act(gates[:, t, :], pT_[:, 128:140], AF.Sigmoid, [("ps", t % 2)], ["gates"])

            if _os.environ.get('K_NSTOP') == 'tok':
                return
            n = 0
            jobs = [(j * 64, 256 + j * 64, qaug[j], ("Q", j)) for j in range(4)] + [(512, 576, kaug, "Kaug"), (640, 704, kwT, "Kw")]
            for cn, cs, dst, dkey in jobs:
                for tc in range(4):
                    tsl = slice(tc * 512, (tc + 1) * 512)
                    bn, bs_ = (n % 2) * 2, (n % 2) * 2 + 1
                    for c in range(8):
                        mm(ps[bn][0:64, :], wfeat[:, c, cn:cn + 64], hT[:, c, tsl], c == 0, c == 7, ["wfeat", ("hT", c, tc)], [("ps", bn)], sig=(c == 7))
                    for c in range(8):
                        mm(ps[bs_][0:64, :], wfeat[:, c, cs:cs + 64], hT[:, c, tsl], c == 0, c == 7, ["wfeat", ("hT", c, tc)], [("ps", bs_)], sig=(c == 7))
                    ta, tb = rtmp[(n % 2) * 2], rtmp[(n % 2) * 2 + 1]
                    tt(ta[0:64, :], ps[bn][0:64, :], rope[0:64, 0, tsl], ALU.mult, [("ps", bn), "RA"], [("rt", (n % 2) * 2)])
                    tt(tb[0:64, :], ps[bs_][0:64, :], rope[0:64, 1, tsl], ALU.mult, [("ps", bs_), "RA"], [("rt", (n % 2) * 2 + 1)])
                    tt(dst[0:64, tsl], ta[0:64, :], tb[0:64, :], ALU.add, [("rt", (n % 2) * 2), ("rt", (n % 2) * 2 + 1)], [dkey], eng="pool")
                    n += 1
            if _os.environ.get('K_NSTOP') == 'feat':
                return
            for kv in range(2):
                cn = 768 + kv * 128
                for tc in range(4):
                    tsl = slice(tc * 512, (tc + 1) * 512)
                    bn = 4 + (n % 2)
                    n += 1
                    for c in range(8):
                        mm(ps[bn][:, :], wfeat[:, c, cn:cn + 128], hT[:, c, tsl], c == 0, c == 7, ["wfeat", ("hT", c, tc)], [("ps", bn)], sig=(c == 7))
                    cp(X2[kv][0:64, tsl], ps[bn][0:64, :], [("ps", bn)], [("X2", kv)])
                    if tc == 0:
                        cp(X2[kv][64:128, 0:511], ps[bn][64:128, 1:512], [("ps", bn)], [("X2", kv)], eng="act")
                    else:
                        cp(X2[kv][64:128, tc * 512 - 1: tc * 512 + 511], ps[bn][64:128, :], [("ps", bn)], [("X2", kv)], eng="act")
            if _os.environ.get('K_NSTOP') == 'x2':
                return
            for kv in range(2):
                for c in range(4):
                    dma("pool", w1[kv][:, c * 4:(c + 1) * 4, :], dr["cmpw1"][(l * 2 + kv) * 128:(l * 2 + kv + 1) * 128, c * 4:(c + 1) * 4, :], (), ["RA"])
            for kv in range(2):
                X3 = X2[kv].rearrange("p (i r) -> p i r", r=16)
                for c in range(16):
                    src = X3[:, 0:127, 2 * c] if c < 8 else X3[:, 1:128, 2 * (c - 8)]
                    ts(Xg[:, c, 0:127], src, pe2[:, kv, c:c + 1], None, ALU.add, None, [("X2", kv), "pe2"], ["Xg"])
                for hh in range(2):
                    for c in range(16):
                        mm(ps[4 + hh][:, 0:127], w1[kv][:, c, hh * 128:(hh + 1) * 128], Xg[:, c, 0:127], c == 0, c == 15, ["RA", "Xg"], [("ps", 4 + hh)], sig=(c == 15))
                    act(hs[:, hh, 0:127], ps[4 + hh][:, 0:127], AF.Silu, [("ps", 4 + hh)], ["hs"])
                if kv == 0:
                    for half_ in range(2):
                        for hh in range(2):
                            mm(ps[6 + half_][0:64, 0:127], w2k[:, hh, half_ * 64:(half_ + 1) * 64], hs[:, hh, 0:127], hh == 0, hh == 1,
                               ["w2k", "hs"], [("ps", 6 + half_)], sig=(hh == 1))
                    tt(rtmp[0][0:64, 0:127], ps[6][0:64, 0:127], ropeC[0:64, 0, :], ALU.mult, [("ps", 6), "ropeC"], [("rt", 0)])
                    tt(rtmp[1][0:64, 0:127], ps[7][0:64, 0:127], ropeC[0:64, 1, :], ALU.mult, [("ps", 7), "ropeC"], [("rt", 1)])
                    tt(kcmpT[0:64, 0:127], rtmp[0][0:64, 0:127], rtmp[1][0:64, 0:127], ALU.add, [("rt", 0), ("rt", 1)], ["Kc"])
                else:
                    for hh in range(2):
                        mm(ps[6][0:127, 0:64], hs[:, hh, 0:127], w2v[:, hh, :], hh == 0, hh == 1, ["w2v", "hs"], [("ps", 6)], sig=(hh == 1))
                    cp(vcaug[0:127, 0:64], ps[6][0:127, 0:64], [("ps", 6)], ["vcaug"])
            S.barrier()

            if _os.environ.get('K_NSTOP') == 'cmp':
                return
            o = o_live
            PTs = [[carve(o + (m * 3 + i) * 1024, [1, 512], BF16)[:, 0, :] for i in range(3)] for m in range(2)]; o += 6144
            PT = PTs[0]
            OTs = [carve(o + m * 2048, [1, 512], F32)[:, 0, :] for m in range(2)]; o += 4096
            accs = [[carve(o + (pp * 4 + j) * 1024, [4, 64], F32) for j in range(4)] for pp in range(2)]; o += 8192
            tmpo = carve(o, [4, 64], F32); o += 1024
            imps = [carve(o + pp * 512, [4, 32], F32) for pp in range(2)]; o += 1024
            scores = [carve(o + pp * 512, [4, 32], F32) for pp in range(2)]; o += 1024
            smc = carve(o, [1, 16], F32)[:, 0, :]; o += 64
            work = carve(o, [1, 32], F32)[:, 0, :]; o += 128
            m8 = carve(o, [1, 16], F32)[:, 0, :]; o += 64
            sm = carve(o, [1, 16], F32)[:, 0, :]; o += 64
            negsts = [carve(o + pp * 1024, [4, 128], BF16) for pp in range(2)]; o += 2048
            tk = carve(o, [32, 32], F32); o += 4096
            cmpb = carve(o, [1, 2048], BF16)[:, 0, :]; o += 4096
            ystage = carve(o, [4, 256], BF16); o += 2048
            yT_sub = carve(o, [2, 2048], BF16); o += 8192
            wo = carve(o, [2, 1024], BF16); o += 4096
            negTs = [carve(o + pp * 1024, [1, 512], BF16)[:, 0, :] for pp in range(2)]; o += 2048
            assert o <= SCRB
            dma("sp", tk[:, :, :], dr["tk"][:, :, :, :].rearrange("p a q n -> p (a q) n"), (), ["tk"])
            dma("pool", cmpb[:, :], dr["cmpbias"][:, :], (), ["cmpb"])
            for pp in range(2):
                memset(negsts[pp][:, :, :], 0.0, [("negst", pp)])
            sc_n = 0.125
            nbx = [0]
            def cmp_topk(Qc):
                par = Qc % 2
                accp, impp, scorep, negstp, negTp = accs[par], imps[par], scores[par], negsts[par], negTs[par]
                q4 = slice(Qc * 4, Qc * 4 + 4)
                for j in range(4):
                    b = nbx[0] % 2
                    csl = slice(Qc * 512, (Qc + 1) * 512)
                    mm(ps[b][0:127, :], kcmpT[0:64, 0:127], qaug[j][0:64, csl], True, False, ["Kc", ("Q", j)], [("ps", b)])
                    mm(ps[b][0:127, :], tri[0:127, 2, 0:127], cmpb[0:127, csl], False, True, ["tri", "cmpb"], [("ps", b)], sig=True)
                    p = nbx[0] % 3
                    nbx[0] += 1
                    act(PT[p][0:127, :], ps[b][0:127, :], AF.Exp, [("ps", b)], [("PT", 0, p)], scale=sc_n)
                    pC = ps[6][:, 0:388].rearrange("p (q e) -> p q e", e=97)
                    for q in range(4):
                        mm(pC[:, q, :], PT[p][0:127, q * 128:(q + 1) * 128], vcaug[0:127, 0:97], q == 0, True,
                           [("PT", 0, p), "vcaug", "vcaug1", "vcaugM"], [("ps", 6)], sig=(q == 3))
                    ts(smc[:, 0:4], pC[:, :, 64], tinyT[:, 0:1], None, ALU.max, None, [("ps", 6), "tiny"], ["smc0"])
                    S.op("dve", lambda e: e.reciprocal(out=smc[:, 4:8], in_=smc[:, 0:4]), ["smc0"], ["smc1"])
                    for q in range(4):
                        if j == 0:
                            ts(impp[:, q, :], pC[:, q, 65:97], smc[:, 4 + q:5 + q], None, ALU.mult, None, [("ps", 6), "smc1"], [("imp", par)])
                        else:
                            stt(impp[:, q, :], pC[:, q, 65:97], smc[:, 4 + q:5 + q], impp[:, q, :], ALU.mult, ALU.add, [("ps", 6), "smc1", ("imp", par)], [("imp", par)])
                    tt(smc[:, 8:12], smc[:, 4:8], gates[:, q4, 3 * j], ALU.mult, ["smc1", "gates"], ["smc2"])
                    tt(accp[j][:, :, :], pC[:, :, 0:64], smc[:, 8:12].unsqueeze(2).to_broadcast([128, 4, 64]), ALU.mult, [("ps", 6), "smc2"], [("acc", par, j)])
                if _os.environ.get('K_ASTOP') == 'ca':
                    return
                tt(scorep[:, :, :], impp[:, :, :], tk[:, Qc * 4:Qc * 4 + 4, :], ALU.mult, [("imp", par), "tk"], [("score", par)])
                tt(scorep[:, :, :], scorep[:, :, :], tk[:, 16 + Qc * 4:16 + Qc * 4 + 4, :], ALU.add, [("score", par), "tk"], [("score", par)])
                for q in range(4):
                    S.op("dve", lambda e, q=q: e.max(out=m8[:, 0:8], in_=scorep[:, q, :]), [("score", par)], ["m8a"])
                    S.op("dve", lambda e, q=q: e.match_replace(out=work[:, :], in_to_replace=m8[:, 0:8], in_values=scorep[:, q, :], imm_value=-2.0),
                         [("score", par), "m8a"], ["work"])
                    S.op("dve", lambda e: e.max(out=m8[:, 8:16], in_=work[:, :]), ["work"], ["m8b"])
                    ts(negstp[:, q, 64:96], scorep[:, q, :], m8[:, 15:16], -BIG, ALU.is_lt, ALU.mult, [("score", par), "m8b"], [("negst", par)])
                if _os.environ.get('K_ASTOP') == 'tk':
                    return
                pb7 = psb(7)
                for q in range(4):
                    S.op("pe", lambda e, q=q: e.transpose(pb7[:, q * 128:(q + 1) * 128], negstp[:, q, :], tri[:, 2, :]),
                         [("negst", par), "tri"], [("ps", 7)], sig=(q == 3))
                cp(negTp[:, :], pb7[:, 0:512], [("ps", 7)], [("negT", par)])
                for j in range(4):
                    dma("sp", qaug[j][64:96, Qc * 512:(Qc + 1) * 512], negTp[64:96, :], [("negT", par)], [("Qn", j)])
                if _os.environ.get('K_ASTOP') == 'nb':
                    return

            def branches(Qc):
                par = Qc % 2
                accp = accs[par]
                q4 = slice(Qc * 4, Qc * 4 + 4)
                kl = []
                for kt in range(max(0, 4 * Qc - 4), 4 * Qc + 4):
                    kl.append((kt, max(kt, 4 * Qc) - 4 * Qc, min(kt + 4, 4 * Qc + 3) - 4 * Qc))

                def wbias(kt, lo, hi, Qc=Qc):
                    r = []
                    if kt >= 4 * Qc:
                        r.append((kt - 4 * Qc, tri[:, 2, :], tri[:, 0, :], "tri"))
                    if kt + 4 <= 4 * Qc + 3:
                        r.append((kt + 4 - 4 * Qc, tri[:, 2, :], tri[:, 1, :], "tri"))
                    return r
                for j in range(4):
                    g1 = attn_stream(0, Qc, causal_klist(Qc),
                                     lambda kt: kaug[0:96, kt * 128:(kt + 1) * 128],
                                     lambda a, b, j=j: qaug[j][0:96, a:b],
                                     lambda kt: vsaug[:, kt, 0:65],
                                     sc_n, PTs[0], (0, 1), 2, causal_bias(Qc), extra=[("Qn", j)])
                    g2 = attn_stream(1, Qc, kl,
                                     lambda kt: kwT[0:64, kt * 128:(kt + 1) * 128],
                                     lambda a, b, j=j: qaug[j][0:64, a:b],
                                     lambda kt: vwaug[:, kt, 0:65],
                                     sc_n, PTs[1], (6, 7), 3, wbias)
                    run_streams([g1, g2])
                    for br in (1, 2):
                        Oacc = finish_stream(1 + br, OTs[br - 1], ("OTs", br - 1), 4)
                        S.op("dve", lambda e: e.reciprocal(out=sm[:, 0:4], in_=Oacc[:, :, 64]), [("ps", 4)], ["sm0"])
                        tt(sm[:, 4:8], sm[:, 0:4], gates[:, q4, 3 * j + br], ALU.mult, ["sm0", "gates"], ["sm1"])
                        tt(tmpo[:, :, :], Oacc[:, :, 0:64], sm[:, 4:8].unsqueeze(2).to_broadcast([128, 4, 64]), ALU.mult, [("ps", 4), "sm1"], ["tmpo"])
                        if br == 1:
                            tt(accp[j][:, :, :], accp[j][:, :, :], tmpo[:, :, :], ALU.add, [("acc", par, j), "tmpo"], [("acc", par, j)], eng="pool")
                        else:
                            tt(ystage[:, :, j * 64:(j + 1) * 64], accp[j][:, :, :], tmpo[:, :, :], ALU.add, [("acc", par, j), "tmpo"], ["ystage"], eng="pool")
                if _os.environ.get('K_ASTOP') == 'att':
                    return
                for q in range(4):
                    t = Qc * 4 + q
                    bank = 5
                    pb_ = psb(bank)
                    for bi in range(2):
                        S.op("pe", lambda e, q=q, bi=bi: e.transpose(pb_[:, bi * 128:(bi + 1) * 128], ystage[:, q, bi * 128:(bi + 1) * 128], tri[:, 2, :]),
                             ["ystage", "tri"], [("ps", bank)], sig=(bi == 1))
                    cp(yT_sub[:, :, t * 128:(t + 1) * 128], pb_[:, 0:256].rearrange("p (c t) -> p c t", c=2), [("ps", bank)], ["yT"])

            cmp_topk(0)
            for Qc in range(4):
                if Qc + 1 < 4:
                    cmp_topk(Qc + 1)
                branches(Qc)
            outproj_partial(l, yT_sub, 2, 4 + 2 * g, wo, "yT")

        for l in range(depth):
            mod_layer(l)
            S.barrier()
            for half in range(2):
                ffn(l, 0, half)
                S.barrier()
            if mix:
                norm_mod(1, [0, 1, 2, 3], modA[:, 8:16], modT[:, 24:32], ("modA", 1), "modT", 0, 4096, 8192)
                S.barrier()
                import os as _os
                if _os.environ.get('K_SKIP_A') is None:
                    mix_A(l)
                    S.barrier()
                for g in range(2):
                    if _os.environ.get('K_SKIP_N') is None:
                        mix_N(l, g)
                        S.barrier()
            for half in range(2):
                ffn(l, 1, half)
                S.barrier()

        if final:
            ostage = [carve(16384 + i * 2048, [1, 512], F32)[:, 0, :] for i in range(4)]
            cnt = [0]

            def out_fn(c, tc, t_, tkey):
                i = cnt[0] % 4
                cnt[0] += 1
                ts(ostage[i], t_, finalg[:, c:c + 1], None, ALU.mult, None, [tkey, "finalg"], [("ost", i)])
                dma("sp", outT_d[c * 128:(c + 1) * 128, tc * 512:(tc + 1) * 512], ostage[i], [("ost", i)], [])
            norm_mod(0, [0, 1, 2, 3], None, None, None, None, 0, 4096, 8192, out_fn=out_fn)
        else:
            for c in range(8):
                for tc in range(4):
                    dma("sp", outT_d[c * 128:(c + 1) * 128, tc * 512:(tc + 1) * 512], xT[:, c, tc * 512:(tc + 1) * 512],
                        [("xT", c, tc)], [])
        S.final_wait()
        print("bass ops:", S.nops)
    return nc


def kernel(_depth=DEPTH, _mix=True, _final=True, **inp):
    inp = {k: np.asarray(v) for k, v in inp.items()}
    shared = _prep_shared(inp, _depth)
    B = inp["x"].shape[0]
    in_maps = []
    for b in range(B):
        m = dict(shared)
        m["xT"] = np.ascontiguousarray(inp["x"][b].T.astype(np.float32))
        m["cT"] = np.ascontiguousarray(inp["c"][b].reshape(8, 128).T.astype(np.float32))
        in_maps.append(m)
    shapes = {k: v.shape for k, v in in_maps[0].items()}
    nc = build(_depth, shapes, mix=_mix, final=_final)
    res = run_bass_kernel_spmd(nc, in_maps, core_ids=list(range(B)))
    out = np.stack([np.ascontiguousarray(res.results[b]["outT"].T) for b in range(B)])
    return out.astype(np.float32)
```
